# Optimizing a Trainium2 kernel written in Bass

```python
import math
import jax, jax.numpy as jnp
from jax import lax
import numpy as np

D_MODEL = 1024
BATCH = 4
SEQ = 8192
DEPTH = 2

MIX_WIDTH = D_MODEL
CONV_A_WIDTH = MIX_WIDTH // 4
DIFF_WIDTH = MIX_WIDTH // 2
CONF_WIDTH = MIX_WIDTH - CONV_A_WIDTH - DIFF_WIDTH
N_DIFF_HEADS = 4
DIFF_HEAD_DIM = DIFF_WIDTH // (2 * N_DIFF_HEADS)
DIFF_V_DIM = 2 * DIFF_HEAD_DIM
SHORT_CONV_K = 3
CONF_CONV_K = 31
ROPE_THETA = 10000.0
Q_BLOCK = 128
FFN_HIDDEN = -(-8 * D_MODEL // (3 * 256)) * 256
RMS_EPS = 1e-6
LN_EPS = 1e-5
IN_SPLITS = [CONV_A_WIDTH, CONV_A_WIDTH, CONV_A_WIDTH,
             DIFF_WIDTH, DIFF_WIDTH, DIFF_WIDTH,
             2 * CONF_WIDTH]
IN_WIDTH = sum(IN_SPLITS)

kernel_name = "hybrid_shortconv_diffattn_conformer"


def rmsnorm(x, g):
    xf = x.astype(jnp.float32)
    y = xf * lax.rsqrt(jnp.mean(xf * xf, axis=-1, keepdims=True) + RMS_EPS)
    return (y * g.astype(jnp.float32)).astype(x.dtype)


def layernorm(x, g, b):
    xf = x.astype(jnp.float32)
    mu = jnp.mean(xf, axis=-1, keepdims=True)
    xc = xf - mu
    y = xc * lax.rsqrt(jnp.mean(xc * xc, axis=-1, keepdims=True) + LN_EPS)
    return (y * g.astype(jnp.float32) + b.astype(jnp.float32)).astype(x.dtype)


def causal_depthwise_conv(x, w):
    k, c = w.shape
    return lax.conv_general_dilated(
        x, w.astype(x.dtype)[:, None, :], window_strides=(1,), padding=((k - 1, 0),),
        dimension_numbers=("NWC", "WIO", "NWC"), feature_group_count=c)


def rope_tables(positions):
    inv_freq = 1.0 / (ROPE_THETA ** (jnp.arange(0, DIFF_HEAD_DIM, 2, dtype=jnp.float32) / DIFF_HEAD_DIM))
    ang = positions.astype(jnp.float32)[..., None] * inv_freq
    return jnp.cos(ang)[:, :, None, None, :], jnp.sin(ang)[:, :, None, None, :]


def apply_rope(x, cos, sin):
    xf = x.astype(jnp.float32)
    x1, x2 = jnp.split(xf, 2, axis=-1)
    out = jnp.concatenate([x1 * cos - x2 * sin, x2 * cos + x1 * sin], axis=-1)
    return out.astype(x.dtype)


def diff_attention(q, k, v, lam):
    b, s, _, h, d = q.shape
    nb = s // Q_BLOCK
    qb = q.reshape(b, nb, Q_BLOCK, 2, h, d).transpose(1, 0, 2, 3, 4, 5)
    key_idx = jnp.arange(s)
    scale = d ** -0.5

    def block(args):
        i, qi = args
        sc = jnp.einsum("bqmhd,bkmhd->bmhqk", qi, k,
                        preferred_element_type=jnp.float32) * scale
        q_idx = i * Q_BLOCK + jnp.arange(Q_BLOCK)
        mask = key_idx[None, :] <= q_idx[:, None]
        p = jax.nn.softmax(jnp.where(mask, sc, -jnp.inf), axis=-1)
        a = p[:, 0] - lam * p[:, 1]
        return jnp.einsum("bhqk,bkhe->bqhe", a.astype(v.dtype), v)

    out = lax.map(block, (jnp.arange(nb), qb))
    return out.transpose(1, 0, 2, 3, 4).reshape(b, s, h, DIFF_V_DIM)


def mixing_layer(h, cos, sin, w_in, short_conv_w, glu_b, conf_dw_w, conf_dw_b,
                 conf_ln_g, conf_ln_b, lam_q1, lam_k1, lam_q2, lam_k2, diff_subln_g,
                 w_out, lam_init):
    b, s, _ = h.shape
    u = jnp.einsum("bsd,de->bse", h, w_in)
    offs = np.cumsum(IN_SPLITS)[:-1].tolist()
    a_b, a_c, a_x, q, k, v, c_in = jnp.split(u, offs, axis=-1)

    y_a = a_b * causal_depthwise_conv(a_c * a_x, short_conv_w)

    q = apply_rope(q.reshape(b, s, 2, N_DIFF_HEADS, DIFF_HEAD_DIM), cos, sin)
    k = apply_rope(k.reshape(b, s, 2, N_DIFF_HEADS, DIFF_HEAD_DIM), cos, sin)
    v = v.reshape(b, s, N_DIFF_HEADS, DIFF_V_DIM)
    lam = (jnp.exp(jnp.sum(lam_q1.astype(jnp.float32) * lam_k1.astype(jnp.float32)))
           - jnp.exp(jnp.sum(lam_q2.astype(jnp.float32) * lam_k2.astype(jnp.float32)))
           + lam_init)
    o = diff_attention(q, k, v, lam)
    o = rmsnorm(o, diff_subln_g) * (1.0 - lam_init)
    y_b = o.reshape(b, s, DIFF_WIDTH)

    c = c_in + glu_b
    c = c[..., :CONF_WIDTH] * jax.nn.sigmoid(c[..., CONF_WIDTH:])
    c = causal_depthwise_conv(c, conf_dw_w) + conf_dw_b
    y_c = jax.nn.silu(layernorm(c, conf_ln_g, conf_ln_b))

    y = jnp.concatenate([y_a, y_b, y_c], axis=-1)
    return jnp.einsum("bse,ed->bsd", y, w_out)


def swiglu(h, w_gate, w_up, w_down):
    g = jnp.einsum("bsd,df->bsf", h, w_gate)
    u = jnp.einsum("bsd,df->bsf", h, w_up)
    return jnp.einsum("bsf,fd->bsd", jax.nn.silu(g) * u, w_down)


def setup_inputs(seed: int = 0) -> dict:
    key = jax.random.key(seed)
    ks = iter(jax.random.split(key, 32))
    f32 = jnp.float32

    def nrm(shape, scale):
        return jax.random.normal(next(ks), shape, f32) * scale

    x = jax.random.normal(next(ks), (BATCH, SEQ, D_MODEL), f32)
    offsets = jax.random.randint(next(ks), (BATCH, 1), 0, 1024, dtype=jnp.int32)
    positions = offsets + jnp.arange(SEQ, dtype=jnp.int32)[None, :]
    return {
        "x": x,
        "positions": positions,
        "mix_norm_g": 1.0 + nrm((DEPTH, D_MODEL), 0.02),
        "w_in": nrm((DEPTH, D_MODEL, IN_WIDTH), D_MODEL ** -0.5),
        "short_conv_w": nrm((DEPTH, SHORT_CONV_K, CONV_A_WIDTH), SHORT_CONV_K ** -0.5),
        "glu_b": nrm((DEPTH, 2 * CONF_WIDTH), 0.02),
        "conf_dw_w": nrm((DEPTH, CONF_CONV_K, CONF_WIDTH), CONF_CONV_K ** -0.5),
        "conf_dw_b": nrm((DEPTH, CONF_WIDTH), 0.02),
        "conf_ln_g": 1.0 + nrm((DEPTH, CONF_WIDTH), 0.02),
        "conf_ln_b": nrm((DEPTH, CONF_WIDTH), 0.02),
        "lam_q1": nrm((DEPTH, DIFF_HEAD_DIM), 0.1),
        "lam_k1": nrm((DEPTH, DIFF_HEAD_DIM), 0.1),
        "lam_q2": nrm((DEPTH, DIFF_HEAD_DIM), 0.1),
        "lam_k2": nrm((DEPTH, DIFF_HEAD_DIM), 0.1),
        "diff_subln_g": 1.0 + nrm((DEPTH, DIFF_V_DIM), 0.02),
        "w_out": nrm((DEPTH, MIX_WIDTH, D_MODEL), MIX_WIDTH ** -0.5),
        "ffn_norm_g": 1.0 + nrm((DEPTH, D_MODEL), 0.02),
        "w_gate": nrm((DEPTH, D_MODEL, FFN_HIDDEN), D_MODEL ** -0.5),
        "w_up": nrm((DEPTH, D_MODEL, FFN_HIDDEN), D_MODEL ** -0.5),
        "w_down": nrm((DEPTH, FFN_HIDDEN, D_MODEL), FFN_HIDDEN ** -0.5),
        "final_norm_g": 1.0 + nrm((D_MODEL,), 0.02),
    }


def reference(x, positions, mix_norm_g, w_in, short_conv_w, glu_b, conf_dw_w, conf_dw_b,
              conf_ln_g, conf_ln_b, lam_q1, lam_k1, lam_q2, lam_k2, diff_subln_g, w_out,
              ffn_norm_g, w_gate, w_up, w_down, final_norm_g):
    cos, sin = rope_tables(positions)
    for l in range(DEPTH):
        lam_init = 0.8 - 0.6 * math.exp(-0.3 * l)
        h = rmsnorm(x, mix_norm_g[l])
        x = x + mixing_layer(h, cos, sin, w_in[l], short_conv_w[l], glu_b[l], conf_dw_w[l],
                             conf_dw_b[l], conf_ln_g[l], conf_ln_b[l], lam_q1[l], lam_k1[l],
                             lam_q2[l], lam_k2[l], diff_subln_g[l], w_out[l], lam_init)
        h = rmsnorm(x, ffn_norm_g[l])
        x = x + swiglu(h, w_gate[l], w_up[l], w_down[l])
    return rmsnorm(x, final_norm_g)
```

```python
import math
from contextlib import ExitStack

import numpy as np
import concourse.bass as bass
import concourse.mybir as mybir
from concourse.bass_utils import run_bass_kernel_spmd

F32 = mybir.dt.float32
BF16 = mybir.dt.bfloat16
I32 = mybir.dt.int32
ALU = mybir.AluOpType
AF = mybir.ActivationFunctionType

D = 1024
DEPTH = 2
FF = 2816
NFC = FF // 128
HL = 2
WIN_EXT = 2048
G_AB, G_AC, G_AX, G_CV, G_CG, G_Q, G_K = 0, 2, 4, 6, 8, 10, 12
V_OFF = 14 * 128
PAIRS = [[0, 1], [2, 3], [4, 5], [6, 7]]
RMS_EPS = 1e-6
LN_EPS = 1e-5
TWO_PI = 2.0 * math.pi
C1 = 6.28125
C2 = TWO_PI - C1


class Sched:
    ENGS = ("pe", "act", "dve", "pool", "sp")

    def __init__(self, nc, n_dma_sems=32, same_engine_sync=True):
        self.nc = nc
        self.same_engine_sync = same_engine_sync
        self.n_dma_sems = n_dma_sems
        self.dma_sems = [nc.alloc_semaphore(f"dq{i}") for i in range(n_dma_sems)]
        self.dma_cnt = [0] * n_dma_sems
        self.dma_last = [None] * n_dma_sems
        self.dma_i = 0
        self.dma_i_sw = 0
        self.phase = -1
        self.res_w = {}
        self.res_r = {}
        self.n_ops = 0
        self.new_phase()

    def new_phase(self):
        self.phase += 1
        self.prog = {e: [] for e in self.ENGS}
        self.esem = {e: self.nc.alloc_semaphore(f"s_{e}_{self.phase}") for e in self.ENGS}
        self.seq = {e: 0 for e in self.ENGS}
        self.sig = {e: 0 for e in self.ENGS}
        self.seq2sig = {e: [] for e in self.ENGS}
        self.pending = {e: [] for e in self.ENGS}
        self.seen = {e: {} for e in self.ENGS}
        for e in self.ENGS:
            for j in range(self.n_dma_sems):
                self.seen[e][("d", j)] = 16 * self.dma_cnt[j]

    def _resolve(self, tok):
        if tok[0] == "d":
            return (("d", tok[1]), self.dma_sems[tok[1]], tok[2])
        _, eng, ph, seq = tok
        v = self.seq2sig[eng][seq]
        assert v is not None, f"dependency on unsignaled instruction {tok}"
        return (("e", eng), self.esem[eng], v)

    def _deps(self, eng, reads, writes):
        toks = []
        for r in reads:
            t = self.res_w.get(r)
            if t is not None:
                toks.append(t)
        for w in writes:
            t = self.res_w.get(w)
            if t is not None:
                toks.append(t)
            toks.extend(self.res_r.get(w, []))
        waits = {}
        for t in toks:
            if t[0] == "e":
                if t[2] != self.phase:
                    continue
                if t[1] == eng and (not self.same_engine_sync or self.seq2sig[eng][t[3]] is None):
                    continue
            key, sem, v = self._resolve(t)
            if self.seen[eng].get(key, 0) >= v:
                continue
            if key not in waits or waits[key][1] < v:
                waits[key] = (sem, v)
        for key, (sem, v) in waits.items():
            self.seen[eng][key] = v
        return list(waits.values())

    def _record(self, tok, reads, writes):
        for r in reads:
            self.res_r.setdefault(r, []).append(tok)
        for w in writes:
            self.res_w[w] = tok
            self.res_r[w] = []

    def op(self, eng, fn, reads=(), writes=(), signal=True):
        waits = self._deps(eng, reads, writes)
        seq = self.seq[eng]
        self.seq[eng] += 1
        tok = ("e", eng, self.phase, seq)
        self.seq2sig[eng].append(None)
        self.pending[eng].append(seq)
        if signal:
            self.sig[eng] += 1
            for s in self.pending[eng]:
                self.seq2sig[eng][s] = self.sig[eng]
            self.pending[eng] = []
        sem = self.esem[eng]

        def thunk(e, fn=fn, waits=waits, signal=signal, sem=sem):
            for (s, v) in waits:
                e.wait_ge(s, v)
            ins = fn(e)
            if signal:
                ins.then_inc(sem, 1)
        self.prog[eng].append(thunk)
        self._record(tok, reads, writes)
        self.n_ops += 1
        return tok

    def dma(self, eng, out, in_, reads=(), writes=(), **kw):
        n_sw = 8
        if eng == "pool":
            j = self.dma_i_sw % n_sw
            self.dma_i_sw += 1
        else:
            j = n_sw + self.dma_i % (self.n_dma_sems - n_sw)
            self.dma_i += 1
        waits = self._deps(eng, reads, writes)
        prev = self.dma_last[j]
        if prev is not None:
            key, sem, v = self._resolve(prev)
            if self.seen[eng].get(key, 0) < v:
                waits.append((sem, v))
                self.seen[eng][key] = v
        self.dma_cnt[j] += 1
        val = 16 * self.dma_cnt[j]
        tok = ("d", j, val)
        self.dma_last[j] = tok
        sem = self.dma_sems[j]

        def thunk(e, waits=waits, sem=sem, out=out, in_=in_, kw=kw):
            for (s, v) in waits:
                e.wait_ge(s, v)
            e.dma_start(out=out, in_=in_, **kw).then_inc(sem, 16)
        self.prog[eng].append(thunk)
        self._record(tok, reads, writes)
        self.n_ops += 1
        return tok

    def collective(self, wait_toks, emit_fn, cc_sem):
        waits = [self._resolve(t)[1:] for t in wait_toks]

        def thunk(e, waits=waits, emit_fn=emit_fn, cc_sem=cc_sem):
            for (s_, v) in waits:
                e.wait_ge(s_, v)
            emit_fn(e).then_inc(cc_sem)
        self.prog["pool"].append(thunk)

    def pool_wait(self, sem, val):
        def thunk(e, sem=sem, val=val):
            e.wait_ge(sem, val)
        self.prog["pool"].append(thunk)

    def end_phase(self):
        for e in self.ENGS:
            assert not self.pending[e], f"unsignaled trailing instructions on {e}"
        targets = []
        for e in self.ENGS:
            if self.sig[e] > 0:
                targets.append((self.esem[e], self.sig[e]))
        for j in range(self.n_dma_sems):
            if self.dma_cnt[j] > 0:
                targets.append((self.dma_sems[j], 16 * self.dma_cnt[j]))
        prog = self.prog
        nc = self.nc

        def run(e, name):
            for t in prog[name]:
                t(e)
            for (s, v) in targets:
                e.wait_ge(s, v)

        with nc.Block() as block:
            @block.tensor
            def _(e):
                run(e, "pe")

            @block.scalar
            def _(e):
                run(e, "act")

            @block.vector
            def _(e):
                run(e, "dve")

            @block.gpsimd
            def _(e):
                run(e, "pool")

            @block.sync
            def _(e):
                run(e, "sp")
        self.res_w = {}
        self.res_r = {}
        self.dma_last = [None] * self.n_dma_sems
        self.new_phase()


def build_nc(S, depth=DEPTH):
    NT = S // 512
    NKB = S // 128
    nc = bass.Bass("TRN2", target_bir_lowering=False)

    def din(name, shape, dt=F32):
        return nc.dram_tensor(name, list(shape), dt, kind="ExternalInput").ap()

    def dscr(name, shape, dt):
        return nc.dram_tensor(name, list(shape), dt, kind="Internal").ap()

    x_in = din("x", [S, D])
    xh_in = din("xh", [S // 2, D])
    rflag = din("rflag", [128, 2])
    pos_in = din("pos", [128, S], I32)
    cst = din("cst", [128, 4])
    gcol = din("gcol", [128, 2 * depth, 8])
    gfin = din("gfin", [128, D])
    gtok = din("gtok", [depth, 128, D])
    w_in = din("w_in", [depth, D, WIN_EXT])
    w_out = din("w_out", [depth, D, D])
    w_gate = din("w_gate", [depth, D, FF])
    w_up = din("w_up", [depth, D, FF])
    w_down = din("w_down", [depth, FF, D])
    scw = din("scw", [depth, 128, 2 * 3])
    glub = din("glub", [depth, 128, 4])
    cdw = din("cdw", [depth, 128, 2 * 31])
    cvec = din("cvec", [depth, 128, 6])
    lamv = din("lamv", [depth, 128, 4 * 64])
    subg = din("subg", [depth, 128, 1])
    out = nc.dram_tensor("out", [S // 2, D], F32, kind="ExternalOutput").ap()
    XCH = 512
    NXC = (S // 2) // XCH
    xsrc = [[nc.dram_tensor(f"xsrc{i}_{c}", [XCH, D], F32) for c in range(NXC)] for i in range(depth - 1)]
    xg = [[nc.dram_tensor(f"xg{i}_{c}", [2 * XCH, D], F32) for c in range(NXC)] for i in range(depth - 1)]

    qT = dscr("qT", [HL, 128, S], BF16)
    kT = dscr("kT", [HL, 128, S], BF16)
    vS = dscr("vS", [S, HL * 128], BF16)
    YC = min(S, 2048)
    NYC = S // YC
    ybsrc = [[[nc.dram_tensor(f"ybsrc{i}_{h}_{c}", [128, YC], BF16) for c in range(NYC)] for h in range(HL)]
             for i in range(depth)]
    ybdst = [[[nc.dram_tensor(f"ybdst{i}_{h}_{c}", [2 * 128, YC], BF16) for c in range(NYC)] for h in range(HL)]
             for i in range(depth)]
    yT = dscr("yT", [512, S], BF16)
    cosT = dscr("cosT", [128, S], F32)
    sinT = dscr("sinT", [128, S], F32)

    sch = Sched(nc)
    op, dma = sch.op, sch.dma

    CH = min(S, 2048)
    es_w0 = ExitStack()
    wsb_pre = es_w0.enter_context(nc.sbuf_tensor("wsb_pre", [128, 8, WIN_EXT], BF16))
    with ExitStack() as es:
        def sb(name, shape, dt=F32):
            return es.enter_context(nc.sbuf_tensor(name, list(shape), dt))
        for k in range(8):
            dma("pool", wsb_pre[:, k, :], w_in[0, k * 128:(k + 1) * 128, :], writes=[("wsb", k)])
        c_sb = sb("c_sb", [128, 4])
        posi = sb("posi", [128, CH], I32)
        ang = sb("ang", [128, CH])
        kf = sb("kf", [128, CH])
        ki = sb("ki", [128, CH], I32)
        r = sb("r", [128, CH])
        m = sb("m", [128, CH])
        rc = sb("rc", [128, CH])
        so = sb("so", [128, CH])
        co = sb("co", [128, CH])
        dma("sp", c_sb[:], cst[:, :], writes=["c_sb"])
        for c in range(S // CH):
            cs = slice(c * CH, (c + 1) * CH)
            dma("sp", posi[:], pos_in[:, cs], writes=["posi"])
            op("dve", lambda e: e.tensor_copy(out=ang[:], in_=posi[:]), reads=["posi"], writes=["ang"])
            op("dve", lambda e: e.tensor_scalar(out=ang[:], in0=ang[:], scalar1=c_sb[:, 0:1], scalar2=None,
                                                op0=ALU.mult), reads=["ang", "c_sb"], writes=["ang"])
            op("dve", lambda e: e.tensor_scalar(out=kf[:], in0=ang[:], scalar1=1.0 / TWO_PI, scalar2=None,
                                                op0=ALU.mult), reads=["ang"], writes=["kf"])
            op("dve", lambda e: e.tensor_copy(out=ki[:], in_=kf[:]), reads=["kf"], writes=["ki"])
            op("dve", lambda e: e.tensor_copy(out=kf[:], in_=ki[:]), reads=["ki"], writes=["kf"])
            op("dve", lambda e: e.scalar_tensor_tensor(out=r[:], in0=kf[:], scalar=-C1, in1=ang[:],
                                                       op0=ALU.mult, op1=ALU.add), reads=["kf", "ang"], writes=["r"])
            op("dve", lambda e: e.scalar_tensor_tensor(out=r[:], in0=kf[:], scalar=-C2, in1=r[:],
                                                       op0=ALU.mult, op1=ALU.add), reads=["kf", "r"], writes=["r"])

            def fold(dst, dn):
                op("dve", lambda e: e.tensor_scalar(out=m[:], in0=dst[:], scalar1=math.pi, scalar2=TWO_PI,
                                                    op0=ALU.is_gt, op1=ALU.mult), reads=[dn], writes=["m"])
                op("dve", lambda e: e.tensor_tensor(out=dst[:], in0=dst[:], in1=m[:], op=ALU.subtract),
                   reads=[dn, "m"], writes=[dn])
                op("dve", lambda e: e.tensor_scalar(out=dst[:], in0=dst[:], scalar1=-math.pi, scalar2=math.pi,
                                                    op0=ALU.max, op1=ALU.min), reads=[dn], writes=[dn])
            fold(r, "r")
            op("dve", lambda e: e.tensor_scalar(out=rc[:], in0=r[:], scalar1=math.pi / 2, scalar2=None,
                                                op0=ALU.add), reads=["r"], writes=["rc"])
            fold(rc, "rc")
            op("dve", lambda e: e.tensor_scalar(out=r[:], in0=r[:], scalar1=c_sb[:, 1:2], scalar2=None,
                                                op0=ALU.mult), reads=["r", "c_sb"], writes=["r"])
            op("act", lambda e: e.activation(out=so[:], in_=r[:], func=AF.Sin), reads=["r"], writes=["so"])
            op("act", lambda e: e.activation(out=co[:], in_=rc[:], func=AF.Sin), reads=["rc"], writes=["co"])
            dma("sp", sinT[:, cs], so[:], reads=["so"], writes=["sinT"])
            dma("sp", cosT[:, cs], co[:], reads=["co"], writes=["cosT"])
        sch.end_phase()

    for l in range(depth):
        def x_tile512(t, l=l):
            if l == 0:
                return x_in.rearrange("(t s p) d -> t p s d", p=128, s=4)[t]
            part, c = divmod(t, NXC)
            return xg[l - 1][c][part * XCH:(part + 1) * XCH, :].rearrange("(s p) d -> p s d", p=128)

        def x_tile256(t, l=l, store=False):
            if store:
                if l == depth - 1:
                    return out.rearrange("(t s p) d -> t p s d", p=128, s=2)[t]
                c, o = divmod(t, XCH // 256)
                return xsrc[l][c][o * 256:(o + 1) * 256, :].rearrange("(s p) d -> p s d", p=128)
            if l == 0:
                return xh_in.rearrange("(t s p) d -> t p s d", p=128, s=2)[t]
            c, o = divmod(t, XCH // 256)
            return xsrc[l - 1][c][o * 256:(o + 1) * 256, :].rearrange("(s p) d -> p s d", p=128)
        lam_init = 0.8 - 0.6 * math.exp(-0.3 * l)
        last = (l == depth - 1)

        with ExitStack() as es:
            def sb(name, shape, dt=F32):
                return es.enter_context(nc.sbuf_tensor(f"{name}_L{l}", list(shape), dt))

            def ps(name, shape, dt=F32):
                return es.enter_context(nc.psum_tensor(f"{name}_L{l}", list(shape), dt))
            wsb = wsb_pre if l == 0 else sb("wsb", [128, 8, WIN_EXT], BF16)
            g_sb = sb("g_sb", [128, 8])
            scw_sb = sb("scw_sb", [128, 6])
            glub_sb = sb("glub_sb", [128, 4])
            cdw_sb = sb("cdw_sb", [128, 62])
            cvec_sb = sb("cvec_sb", [128, 6])
            identf = sb("identf", [128, 128])
            ident = sb("ident", [128, 128], BF16)
            diag = sb("diag", [128, 62, 128], BF16)
            onesM = sb("onesM", [128, 128])
            pswap = sb("pswap", [128, 128], BF16)
            qb = [sb(f"qb{i}", [128, 512], BF16) for i in range(2)]
            mh = sb("mh", [128, 512])
            xt = [sb(f"xt{i}", [128, 4, D]) for i in range(2)]
            ss = [sb(f"ss{i}", [128, 4]) for i in range(2)]
            rs = [sb(f"rs{i}", [128, 4]) for i in range(2)]
            hb = [sb(f"hb{i}", [128, D], BF16) for i in range(2)]
            gt_sb = sb("gt_sb", [128, D])
            junk = sb("junk", [128, D], BF16)
            hT = [sb(f"hT{i}", [128, 8, 512], BF16) for i in range(2)]
            cs_sb = [sb(f"cos{i}", [128, 512]) for i in range(2)]
            sn_sb = [sb(f"sin{i}", [128, 512]) for i in range(2)]
            zc = [sb(f"zc{c}", [128, 2 + 512]) for c in range(2)]
            cbuf = [sb(f"cbuf{c}", [128, 32 + 512], BF16) for c in range(2)]
            tA = sb("tA", [128, 512])
            tB = sb("tB", [128, 512])
            tS = sb("tS", [128, 512])
            t1 = [sb(f"t1_{i}", [128, 512]) for i in range(2)]
            t2 = [sb(f"t2_{i}", [128, 512]) for i in range(2)]
            cv = [[sb(f"cv{i}_{c}", [128, 512]) for c in range(2)] for i in range(2)]
            sq = [sb(f"sq{c}", [128, 512]) for c in range(2)]
            rstd = sb("rstd", [128, 512])
            ob = [sb(f"ob{i}", [128, 512], BF16) for i in range(4)]
            vb = [sb(f"vb{i}", [128, HL * 128], BF16) for i in range(2)]
            pT = [ps(f"pT{i}", [128, D], BF16) for i in range(2)]
            pp = [ps(f"pp{i}", [128, 512]) for i in range(4)]
            pcv = ps("pcv", [128, 512])
            pst = ps("pst", [128, 512])

            for k in range(8 if l > 0 else 0):
                dma("pool", wsb[:, k, :], w_in[l, k * 128:(k + 1) * 128, :], writes=[("wsb", k)])
            dma("sp", gt_sb[:], gtok[l], writes=["gt_sb"])
            dma("sp", scw_sb[:], scw[l], writes=["scw_sb"])
            dma("sp", glub_sb[:], glub[l], writes=["glub_sb"])
            dma("sp", cdw_sb[:], cdw[l], writes=["cdw_sb"])
            dma("sp", cvec_sb[:], cvec[l], writes=["cvec_sb"])
            op("pool", lambda e: e.memset(identf[:], 1.0), writes=["identf"])
            op("pool", lambda e: e.affine_select(out=identf[:], in_=identf[:], pattern=[[-1, 128]],
                                                 compare_op=ALU.is_equal, fill=0.0, base=0,
                                                 channel_multiplier=1), reads=["identf"], writes=["identf"])
            op("dve", lambda e: e.tensor_copy(out=ident[:], in_=identf[:]), reads=["identf"], writes=["ident"])
            op("pool", lambda e: e.memset(onesM[:], 1.0 / 256.0), writes=["onesM"])
            op("pool", lambda e: e.memset(mh[:], -0.5), writes=["mh"])
            for b64 in range(2):
                for hh in range(2):
                    d0 = 64 * b64 + 32 * hh
                    s0 = 64 * b64 + 32 * (1 - hh)
                    op("dve", lambda e, d0=d0, s0=s0: e.tensor_copy(out=pswap[:, d0:d0 + 32], in_=identf[:, s0:s0 + 32]),
                       reads=["identf"], writes=["pswap"])
            for c in range(2):
                op("pool", lambda e, c=c: e.memset(zc[c][:, 0:2], 0.0), writes=[("zc", c)])
                op("pool", lambda e, c=c: e.memset(cbuf[c][:, 0:32], 0.0), writes=[("cbuf", c)])
            for i in range(62):
                op("dve", lambda e, i=i: e.tensor_scalar(out=diag[:, i, :], in0=identf[:], scalar1=cdw_sb[:, i:i + 1],
                                                         scalar2=None, op0=ALU.mult),
                   reads=["identf", "cdw_sb"], writes=["diag"])

            pp_i = [0]
            ob_i = [0]

            def ab_load(t, x=True, cs=True):
                sl = t % 2
                if x:
                    dma("sp", xt[sl][:], x_tile512(t), writes=[("xt", sl)])
                if cs:
                    dma("sp", cs_sb[sl][:], cosT[:, t * 512:(t + 1) * 512], writes=[("cos", sl)])
                    dma("sp", sn_sb[sl][:], sinT[:, t * 512:(t + 1) * 512], writes=[("sin", sl)])

            def ab_stats(t):
                sl = t % 2
                for s in range(4):
                    op("act", lambda e, s=s: e.activation(out=junk[:], in_=xt[sl][:, s, :], func=AF.Square,
                                                          accum_out=ss[sl][:, s:s + 1]),
                       reads=[("xt", sl)], writes=["junk", ("ss", sl)])
                op("dve", lambda e: e.tensor_scalar(out=rs[sl][:], in0=ss[sl][:], scalar1=1.0 / D, scalar2=RMS_EPS,
                                                    op0=ALU.mult, op1=ALU.add),
                   reads=[("ss", sl)], writes=[("rs", sl)])
                op("pool", lambda e: e.tensor_tensor(out=rs[sl][:], in0=rs[sl][:], in1=mh[:, 0:4], op=ALU.pow),
                   reads=[("rs", sl), "mh"], writes=[("rs", sl)])

            def ab_sub_a(t, s):
                sl = t % 2
                b_ = s % 2
                op("dve", lambda e: e.scalar_tensor_tensor(out=hb[b_][:], in0=xt[sl][:, s, :], scalar=rs[sl][:, s:s + 1],
                                                           in1=gt_sb[:], op0=ALU.mult, op1=ALU.mult),
                   reads=[("xt", sl), ("rs", sl), "gt_sb"], writes=[("hb", b_)])

            def ab_sub(t, s, with_a=True):
                sl = t % 2
                b_ = s % 2
                if with_a:
                    ab_sub_a(t, s)
                for k in range(8):
                    op("pe", lambda e, k=k: e.transpose(out=pT[b_][:, k * 128:(k + 1) * 128],
                                                        in_=hb[b_][:, k * 128:(k + 1) * 128], identity=ident[:]),
                       reads=[("hb", b_), "ident"], writes=[("pT", b_)], signal=(k == 7))
                op("act", lambda e: e.copy(out=hT[sl][:, :, s * 128:(s + 1) * 128],
                                           in_=pT[b_][:].rearrange("p (k c) -> p k c", k=8)),
                   reads=[("pT", b_)], writes=[("hT", sl)])

            def proj(t, gi):
                sl = t % 2
                i = pp_i[0] % 4
                pp_i[0] += 1
                for k in range(8):
                    op("pe", lambda e, k=k, i=i: e.matmul(pp[i][:], lhsT=wsb[:, k, gi * 128:(gi + 1) * 128],
                                                          rhs=hT[sl][:, k, :], start=(k == 0), stop=(k == 7)),
                       reads=[("hT", sl), ("wsb", k)], writes=[("pp", i)], signal=(k == 7))
                return pp[i], ("pp", i)

            def c1(t, hooks):
                sl = t % 2
                cols = slice(t * 512, (t + 1) * 512)
                gcount = [0]

                def hook():
                    for f in hooks.get(gcount[0], []):
                        f()
                    gcount[0] += 1
                for c in range(2):
                    pg, rg = proj(t, G_CG + c)
                    op("act", lambda e, pg=pg, c=c: e.activation(out=tS[:], in_=pg[:], func=AF.Sigmoid,
                                                                 bias=glub_sb[:, 2 + c:3 + c]),
                       reads=[rg, "glub_sb"], writes=["tS"])
                    hook()
                    pvl, rv = proj(t, G_CV + c)
                    op("dve", lambda e, pvl=pvl, c=c: e.scalar_tensor_tensor(
                        out=cbuf[c][:, 32:544], in0=pvl[:], scalar=glub_sb[:, c:c + 1], in1=tS[:],
                        op0=ALU.add, op1=ALU.mult), reads=[rv, "tS", "glub_sb"], writes=[("cbuf", c)])
                    hook()
                for c in range(2):
                    pc, rcn = proj(t, G_AC + c)
                    op("act", lambda e, pc=pc: e.copy(out=tA[:], in_=pc[:]), reads=[rcn], writes=["tA"])
                    hook()
                    px, rx = proj(t, G_AX + c)
                    op("dve", lambda e, px=px, c=c: e.tensor_tensor(out=zc[c][:, 2:514], in0=px[:], in1=tA[:],
                                                                    op=ALU.mult),
                       reads=[rx, "tA"], writes=[("zc", c)])
                    op("dve", lambda e, c=c: e.tensor_scalar(out=tB[:], in0=zc[c][:, 0:512],
                                                             scalar1=scw_sb[:, 3 * c:3 * c + 1], scalar2=None,
                                                             op0=ALU.mult),
                       reads=[("zc", c), "scw_sb"], writes=["tB"])
                    for j in (1, 2):
                        op("dve", lambda e, c=c, j=j: e.scalar_tensor_tensor(
                            out=tB[:], in0=zc[c][:, j:j + 512], scalar=scw_sb[:, 3 * c + j:3 * c + j + 1],
                            in1=tB[:], op0=ALU.mult, op1=ALU.add),
                           reads=[("zc", c), "scw_sb", "tB"], writes=["tB"])
                    op("dve", lambda e, c=c: e.tensor_copy(out=zc[c][:, 0:2], in_=zc[c][:, 512:514]),
                       reads=[("zc", c)], writes=[("zc", c)])
                    hook()
                    pb_, rb = proj(t, G_AB + c)
                    o = ob_i[0] % 4
                    ob_i[0] += 1
                    op("dve", lambda e, pb_=pb_, o=o: e.tensor_tensor(out=ob[o][:], in0=pb_[:], in1=tB[:], op=ALU.mult),
                       reads=[rb, "tB"], writes=[("ob", o)])
                    dma("sp", yT[c * 128:(c + 1) * 128, cols], ob[o][:], reads=[("ob", o)], writes=["yT"])
                    hook()
                pend = [None]

                def rope_tail(pq, rq, h, i2, dst, qi):
                    def f():
                        i = pp_i[0] % 4
                        pp_i[0] += 1
                        op("pe", lambda e, i=i: e.matmul(pp[i][:], lhsT=pswap[:], rhs=qb[qi][:], start=True, stop=True),
                           reads=["pswap", ("qb", qi)], writes=[("pp", i)])
                        op("dve", lambda e, i=i: e.tensor_tensor(out=t2[i2][:], in0=pp[i][:], in1=sn_sb[sl][:], op=ALU.mult),
                           reads=[("pp", i), ("sin", sl)], writes=[("t2", i2)])
                        o = ob_i[0] % 4
                        ob_i[0] += 1
                        op("pool", lambda e, o=o: e.tensor_tensor(out=ob[o][:], in0=t1[i2][:], in1=t2[i2][:], op=ALU.add),
                           reads=[("t1", i2), ("t2", i2)], writes=[("ob", o)])
                        dma("sp", dst[h, :, cols], ob[o][:], reads=[("ob", o)], writes=[dst.name])
                    return f
                qcnt = 0
                for (g0, dst) in ((G_Q, qT), (G_K, kT)):
                    for h in range(HL):
                        i2 = qcnt % 2
                        qi = qcnt % 2
                        qcnt += 1
                        pq, rq = proj(t, g0 + h)
                        op("dve", lambda e, pq=pq, qi=qi: e.tensor_copy(out=qb[qi][:], in_=pq[:]), reads=[rq],
                           writes=[("qb", qi)])
                        op("dve", lambda e, pq=pq, i2=i2: e.tensor_tensor(out=t1[i2][:], in0=pq[:], in1=cs_sb[sl][:],
                                                                          op=ALU.mult),
                           reads=[rq, ("cos", sl), ("qb", qi)], writes=[("t1", i2)])
                        if pend[0] is not None:
                            pend[0]()
                        pend[0] = rope_tail(pq, rq, h, i2, dst, qi)
                        for _ in range(4):
                            hook()
                pend[0]()
                for s in range(4):
                    i = pp_i[0] % 4
                    pp_i[0] += 1
                    for k in range(8):
                        op("pe", lambda e, k=k, s=s, i=i: e.matmul(pp[i][:, 0:HL * 128], lhsT=hT[sl][:, k, s * 128:(s + 1) * 128],
                                                                   rhs=wsb[:, k, V_OFF:V_OFF + HL * 128],
                                                                   start=(k == 0), stop=(k == 7)),
                           reads=[("hT", sl), ("wsb", k)], writes=[("pp", i)], signal=(k == 7))
                    vi = s % 2
                    op("act", lambda e, vi=vi, i=i: e.copy(out=vb[vi][:], in_=pp[i][:, 0:HL * 128]), reads=[("pp", i)],
                       writes=[("vb", vi)])
                    r0 = t * 512 + s * 128
                    dma("sp", vS[r0:r0 + 128, :], vb[vi][:], reads=[("vb", vi)], writes=["vS"])
                    hook()
                for c in range(2):
                    for j in range(31):
                        op("pe", lambda e, c=c, j=j: e.matmul(pcv[:], lhsT=diag[:, c * 31 + j, :],
                                                              rhs=cbuf[c][:, 2 + j:2 + j + 512],
                                                              start=(j == 0), stop=(j == 30)),
                           reads=["diag", ("cbuf", c)], writes=["pcv"], signal=(j == 30))
                    op("act", lambda e, c=c: e.activation(out=cv[sl][c][:], in_=pcv[:], func=AF.Identity,
                                                          bias=cvec_sb[:, c:c + 1]),
                       reads=["pcv", "cvec_sb"], writes=[("cv", sl, c)])
                    op("pool", lambda e, c=c: e.tensor_copy(out=cbuf[c][:, 0:32], in_=cbuf[c][:, 512:544]),
                       reads=[("cbuf", c)], writes=[("cbuf", c)])
                    hook()
                while any(k >= gcount[0] for k in hooks):
                    hook()

            def c2_pieces(t):
                sl = t % 2
                cols = slice(t * 512, (t + 1) * 512)

                def p0():
                    for c in range(2):
                        op("pe", lambda e, c=c: e.matmul(pst[:], lhsT=onesM[:], rhs=cv[sl][c][:], start=(c == 0),
                                                         stop=(c == 1)),
                           reads=["onesM", ("cv", sl, c)], writes=["pst"], signal=(c == 1))
                    for c in range(2):
                        op("dve", lambda e, c=c: e.tensor_tensor(out=cv[sl][c][:], in0=cv[sl][c][:], in1=pst[:],
                                                                 op=ALU.subtract),
                           reads=[("cv", sl, c), "pst"], writes=[("cv", sl, c)])
                        op("act", lambda e, c=c: e.activation(out=sq[c][:], in_=cv[sl][c][:], func=AF.Square),
                           reads=[("cv", sl, c)], writes=[("sq", c)])

                def p1():
                    for c in range(2):
                        op("pe", lambda e, c=c: e.matmul(pst[:], lhsT=onesM[:], rhs=sq[c][:], start=(c == 0),
                                                         stop=(c == 1)),
                           reads=["onesM", ("sq", c)], writes=["pst"], signal=(c == 1))
                    op("dve", lambda e: e.tensor_scalar(out=rstd[:], in0=pst[:], scalar1=LN_EPS, scalar2=None,
                                                        op0=ALU.add), reads=["pst"], writes=["rstd"])
                    op("act", lambda e: e.activation(out=rstd[:], in_=rstd[:], func=AF.Sqrt), reads=["rstd"],
                       writes=["rstd"])
                    op("dve", lambda e: e.reciprocal(out=rstd[:], in_=rstd[:]), reads=["rstd"], writes=["rstd"])

                def p2():
                    for c in range(2):
                        op("dve", lambda e, c=c: e.tensor_tensor(out=cv[sl][c][:], in0=cv[sl][c][:], in1=rstd[:],
                                                                 op=ALU.mult),
                           reads=[("cv", sl, c), "rstd"], writes=[("cv", sl, c)])
                        o = ob_i[0] % 4
                        ob_i[0] += 1
                        op("act", lambda e, c=c, o=o: e.activation(out=ob[o][:], in_=cv[sl][c][:], func=AF.Silu,
                                                                   scale=cvec_sb[:, 2 + c:3 + c],
                                                                   bias=cvec_sb[:, 4 + c:5 + c]),
                           reads=[("cv", sl, c), "cvec_sb"], writes=[("ob", o)])
                        dma("sp", yT[256 + c * 128:256 + (c + 1) * 128, cols], ob[o][:], reads=[("ob", o)],
                            writes=["yT"])
                return p0, p1, p2

            ab_load(0)
            ab_stats(0)
            for s in range(4):
                ab_sub(0, s)
            if NT > 1:
                ab_load(1)
            for t in range(NT):
                hooks = {}

                def add(gi, f):
                    hooks.setdefault(gi, []).append(f)
                if t >= 1:
                    p0, p1, p2 = c2_pieces(t - 1)
                    add(1, p0)
                    add(6, p1)
                    add(11, p2)
                if t + 1 < NT:
                    add(3, lambda t=t: ab_stats(t + 1))
                    for s in range(4):
                        add(11 + 4 * s, lambda t=t, s=s: ab_sub_a(t + 1, s))
                        add(15 + 4 * s, lambda t=t, s=s: ab_sub(t + 1, s, with_a=False))
                if t + 2 < NT:
                    add(13, lambda t=t: ab_load(t + 2, cs=False))
                c1(t, hooks)
                if t + 2 < NT:
                    ab_load(t + 2, x=False)
            for f in c2_pieces(NT - 1):
                f()
            sch.end_phase()
        if l == 0:
            es_w0.close()


        es_w = ExitStack()
        wo = es_w.enter_context(nc.sbuf_tensor(f"wo_L{l}", [128, 8, D], BF16))
        wg = es_w.enter_context(nc.sbuf_tensor(f"wg_L{l}", [128, 8, FF], BF16))
        with ExitStack() as es:
            def sb(name, shape, dt=F32):
                return es.enter_context(nc.sbuf_tensor(f"{name}_L{l}", list(shape), dt))

            def ps(name, shape, dt=F32):
                return es.enter_context(nc.psum_tensor(f"{name}_L{l}", list(shape), dt))
            ksb = [sb(f"ksb{i}", [128, S], BF16) for i in range(2)]
            vsb = [sb(f"vsb{i}", [128, NKB, 128], BF16) for i in range(2)]
            qsb = [sb(f"qsb{i}", [128, 512], BF16) for i in range(3)]
            pb = [sb(f"pb{i}", [128, 2, 512], BF16) for i in range(3)]
            trif = sb("trif", [128, 128])
            tri2 = sb("tri2", [128, 2, 128], BF16)
            ones_b = sb("ones_b", [128, 128], BF16)
            onesE = sb("onesE", [128, 128])
            ones_f = sb("ones_f", [128, 128])
            acc2 = [sb(f"acc2_{i}", [128, 512]) for i in range(2)]
            epsc = sb("epsc", [128, 1])
            lam_sb = sb("lam_sb", [128, 256])
            lprod = sb("lprod", [128, 2, 64])
            lsum = sb("lsum", [128, 2])
            nlam = sb("nlam", [128, 1])
            sg_sb = sb("sg_sb", [128, 1])
            osb = [sb(f"osb{i}", [128, 2, 512]) for i in range(2)]
            lsb = [sb(f"lsb{i}", [128, 2, 512]) for i in range(2)]
            o1 = [sb(f"o1_{i}", [128, 512]) for i in range(2)]
            osq = [sb(f"osq{i}", [128, 512]) for i in range(2)]
            rstd = [sb(f"rstd2_{i}", [128, 512]) for i in range(2)]
            yb = [sb(f"yb{i}", [128, 512], BF16) for i in range(2)]
            ps_s = [ps(f"ps_s{i}", [128, 2, 512]) for i in range(2)]
            ps_o = ps("ps_o", [128, 2, 512])
            ps_l = ps("ps_l", [128, 2, 512])

            dma("sp", lam_sb[:], lamv[l], writes=["lam_sb"])
            dma("sp", sg_sb[:], subg[l], writes=["sg_sb"])
            op("pool", lambda e: e.memset(trif[:], 1.0), writes=["trif"])
            op("pool", lambda e: e.affine_select(out=trif[:], in_=trif[:], pattern=[[1, 128]],
                                                 compare_op=ALU.is_ge, fill=0.0, base=0,
                                                 channel_multiplier=-1), reads=["trif"], writes=["trif"])
            for mm in range(2):
                op("dve", lambda e, mm=mm: e.tensor_copy(out=tri2[:, mm, :], in_=trif[:]), reads=["trif"], writes=["tri2"])
            op("pool", lambda e: e.memset(ones_b[:], 1.0), writes=["ones_b"])
            op("pool", lambda e: e.memset(onesE[:], 1.0 / 128.0), writes=["onesE"])
            op("pool", lambda e: e.memset(ones_f[:], 1.0), writes=["ones_f"])
            op("pool", lambda e: e.memset(epsc[:], RMS_EPS), writes=["epsc"])
            lv = lam_sb[:].rearrange("p (a d) -> p a d", a=4)
            for i in range(2):
                op("dve", lambda e, i=i: e.tensor_tensor(out=lprod[:, i, :], in0=lv[:, 2 * i, :], in1=lv[:, 2 * i + 1, :],
                                                         op=ALU.mult), reads=["lam_sb"], writes=["lprod"])
            op("dve", lambda e: e.tensor_reduce(out=lsum[:], in_=lprod[:], axis=mybir.AxisListType.X, op=ALU.add),
               reads=["lprod"], writes=["lsum"])
            op("act", lambda e: e.activation(out=lsum[:], in_=lsum[:], func=AF.Exp), reads=["lsum"], writes=["lsum"])
            op("dve", lambda e: e.tensor_tensor(out=nlam[:], in0=lsum[:, 1:2], in1=lsum[:, 0:1], op=ALU.subtract),
               reads=["lsum"], writes=["nlam"])
            op("dve", lambda e: e.tensor_scalar(out=nlam[:], in0=nlam[:], scalar1=-lam_init, scalar2=None, op0=ALU.add),
               reads=["nlam"], writes=["nlam"])
            op("dve", lambda e: e.tensor_scalar(out=sg_sb[:], in0=sg_sb[:], scalar1=1.0 - lam_init, scalar2=None,
                                                op0=ALU.mult), reads=["sg_sb"], writes=["sg_sb"])

            def load_head(h):
                hs = h % 2
                dma("sp", ksb[hs][:], kT[h], writes=[("ksb", hs)])
                dma("sp", vsb[hs][:], vS[:, h * 128:(h + 1) * 128].rearrange("(kb p) e -> p kb e", p=128),
                    writes=[("vsb", hs)])

            def load_q(gidx):
                h, g = divmod(gidx, NT)
                qs = gidx % 3
                dma("sp", qsb[qs][:], qT[h, :, g * 512:(g + 1) * 512], writes=[("qsb", qs)])

            steps = []
            for gidx in range(HL * NT):
                g = gidx % NT
                for kb in range(4 * g + 4):
                    steps.append((gidx, kb))
            slot_of = {}
            cnt = {"si": 0, "pi": 0, "step": 0}
            deferred = []

            def emit_S(i):
                gidx, kb = steps[i]
                h, g = divmod(gidx, NT)
                hs, qs = h % 2, gidx % 3
                j = kb - 4 * g
                q0 = 128 * j if j > 0 else 0
                s_ = cnt["si"] % 2
                cnt["si"] += 1
                slot_of[i] = s_
                for mm in range(2):
                    lo = 64 * mm
                    op("pe", lambda e, mm=mm, lo=lo: e.matmul(
                        ps_s[s_][:, mm, q0:512], lhsT=ksb[hs][lo:lo + 64, kb * 128:(kb + 1) * 128],
                        rhs=qsb[qs][lo:lo + 64, q0:512], start=True, stop=True),
                       reads=[("ksb", hs), ("qsb", qs)], writes=[("ps_s", s_)], signal=(mm == 1))

            def emit_rest(i):
                gidx, kb = steps[i]
                h, g = divmod(gidx, NT)
                hs = h % 2
                nkb = 4 * g + 4
                j = kb - 4 * g
                q0 = 128 * j if j > 0 else 0
                s_ = slot_of.pop(i)
                p_ = cnt["pi"] % 3
                cnt["pi"] += 1
                op("act", lambda e: e.activation(out=pb[p_][:, :, q0:512], in_=ps_s[s_][:, :, q0:512],
                                                 func=AF.Exp, scale=0.125),
                   reads=[("ps_s", s_)], writes=[("pb", p_)])
                if j >= 0:
                    op("dve", lambda e: e.tensor_tensor(out=pb[p_][:, :, q0:q0 + 128], in0=pb[p_][:, :, q0:q0 + 128],
                                                        in1=tri2[:], op=ALU.mult),
                       reads=[("pb", p_), "tri2"], writes=[("pb", p_)])
                run_deferred()
                if i + 2 < len(steps):
                    emit_S(i + 2)
                first = (kb == 0)
                lastk = (kb == nkb - 1)
                for mm in range(2):
                    op("pe", lambda e, mm=mm: e.matmul(ps_o[:, mm, q0:512], lhsT=vsb[hs][:, kb, :],
                                                       rhs=pb[p_][:, mm, q0:512], start=first, stop=lastk),
                       reads=[("vsb", hs), ("pb", p_)], writes=["ps_o"], signal=False)
                op("pe", lambda e: e.matmul(ps_l[:, 0, q0:512], lhsT=ones_b[:], rhs=pb[p_][:, 0, q0:512],
                                            start=first, stop=lastk),
                   reads=["ones_b", ("pb", p_)], writes=["ps_l"], signal=True)
                pr = gidx % 2
                if first:
                    op("dve", lambda e: e.tensor_copy(out=acc2[pr][:], in_=pb[p_][:, 1, :]),
                       reads=[("pb", p_)], writes=[("acc2", pr)])
                else:
                    op("dve", lambda e: e.tensor_tensor(out=acc2[pr][:, q0:512], in0=acc2[pr][:, q0:512],
                                                        in1=pb[p_][:, 1, q0:512], op=ALU.add),
                       reads=[("pb", p_), ("acc2", pr)], writes=[("acc2", pr)])
                if lastk:
                    op("pe", lambda e: e.matmul(ps_l[:, 1, :], lhsT=ones_f[:], rhs=acc2[pr][:], start=True, stop=True),
                       reads=["ones_f", ("acc2", pr)], writes=["ps_l"], signal=True)
                    epilogue(gidx)

            def epilogue(gidx):
                h, g = divmod(gidx, NT)
                pr = gidx % 2
                op("act", lambda e: e.activation(out=lsb[pr][:], in_=ps_l[:], func=AF.Ln),
                   reads=["ps_l"], writes=[("lsb", pr)])
                op("act", lambda e: e.activation(out=lsb[pr][:], in_=lsb[pr][:], func=AF.Exp, scale=-1.0),
                   reads=[("lsb", pr)], writes=[("lsb", pr)])
                op("dve", lambda e: e.tensor_tensor(out=osb[pr][:], in0=ps_o[:], in1=lsb[pr][:], op=ALU.mult),
                   reads=["ps_o", ("lsb", pr)], writes=[("osb", pr)])
                op("dve", lambda e: e.scalar_tensor_tensor(out=o1[pr][:], in0=osb[pr][:, 1, :], scalar=nlam[:, 0:1],
                                                           in1=osb[pr][:, 0, :], op0=ALU.mult, op1=ALU.add),
                   reads=[("osb", pr), "nlam"], writes=[("o1", pr)])
                op("dve", lambda e: e.tensor_tensor(out=osq[pr][:], in0=o1[pr][:], in1=o1[pr][:], op=ALU.mult),
                   reads=[("o1", pr)], writes=[("osq", pr)])

                def part_c1():
                    s_ = cnt["si"] % 2
                    op("pe", lambda e: e.matmul(ps_s[s_][:, 0, :], lhsT=onesE[:], rhs=osq[pr][:], start=True, stop=True),
                       reads=["onesE", ("osq", pr)], writes=[("ps_s", s_)])

                    def part_c2():
                        op("act", lambda e: e.activation(out=rstd[pr][:], in_=ps_s[s_][:, 0, :], func=AF.Ln,
                                                         bias=epsc[:, 0:1]),
                           reads=[("ps_s", s_), "epsc"], writes=[("rstd", pr)])
                        op("act", lambda e: e.activation(out=rstd[pr][:], in_=rstd[pr][:], func=AF.Exp, scale=-0.5),
                           reads=[("rstd", pr)], writes=[("rstd", pr)])
                        op("dve", lambda e: e.scalar_tensor_tensor(out=yb[pr][:], in0=o1[pr][:], scalar=sg_sb[:, 0:1],
                                                                   in1=rstd[pr][:], op0=ALU.mult, op1=ALU.mult),
                           reads=[("o1", pr), "sg_sb", ("rstd", pr)], writes=[("yb", pr)])
                        gpc = YC // 512
                        yc, go = divmod(g, gpc)
                        tk = dma("sp", ybsrc[l][h][yc][:, go * 512:(go + 1) * 512], yb[pr][:],
                                 reads=[("yb", pr)], writes=["ybsrc"])
                        yb_toks[h][yc].append(tk)
                        if len(yb_toks[h][yc]) == gpc:
                            sch.collective(yb_toks[h][yc], lambda e, h=h, yc=yc: e.collective_compute(
                                "AllGather", ALU.bypass, replica_groups=PAIRS,
                                ins=[ybsrc[l][h][yc][:, :]], outs=[ybdst[l][h][yc][:, :]]), cc_sem)
                    part_c2()
                deferred.append((cnt["step"] + 10, part_c1))

            def run_deferred(force=False):
                while True:
                    due = [d for d in deferred if force or d[0] <= cnt["step"]]
                    if not due:
                        break
                    d = due[0]
                    deferred.remove(d)
                    d[1]()

            yb_toks = [[[] for _ in range(NYC)] for _ in range(HL)]
            cc_sem = nc.alloc_semaphore(f"cc_sem{l}")
            for k in range(8):
                dma("pool", wo[:, k, :], w_out[l, k * 128:(k + 1) * 128, :], writes=[("wo", k)])
            for k in range(8):
                dma("pool", wg[:, k, :], w_gate[l, k * 128:(k + 1) * 128, :], writes=[("wg", k)])
            load_head(0)
            load_q(0)
            load_q(1)
            emit_S(0)
            emit_S(1)
            for i, (gidx, kb) in enumerate(steps):
                cnt["step"] = i
                h, g = divmod(gidx, NT)
                if kb == 0:
                    if g == 0 and h + 1 < HL:
                        load_head(h + 1)
                    if gidx + 2 < HL * NT:
                        load_q(gidx + 2)
                emit_rest(i)
            cnt["step"] += 1000
            run_deferred(force=True)
            sch.pool_wait(cc_sem, HL * NYC)
            op("pool", lambda e: e.memset(epsc[:], RMS_EPS), writes=["epsc"])
            sch.end_phase()

        with ExitStack() as es:
            def sb(name, shape, dt=F32):
                return es.enter_context(nc.sbuf_tensor(f"{name}_L{l}", list(shape), dt))

            def ps(name, shape, dt=F32):
                return es.enter_context(nc.psum_tensor(f"{name}_L{l}", list(shape), dt))
            TT = 256
            NT3 = (S // 2) // TT
            wu = sb("wu", [128, 8, FF], BF16)
            wd = sb("wd", [128, NFC, D], BF16)
            g_sb = sb("g3_sb", [128, 8])
            gf_sb = sb("gf_sb", [128, D]) if last else None
            identf = sb("identf3", [128, 128])
            ident = sb("ident3", [128, 128], BF16)
            mh = sb("mh3", [128, 4])
            xt = [sb(f"x3_{i}", [128, 2, D]) for i in range(2)]
            yt = [sb(f"y3_{i}", [128, 8, TT], BF16) for i in range(2)]
            ytc = sb("y3c", [128, 8, TT], BF16)
            rf_sb = sb("rf_sb", [128, 2])
            ss = sb("ss3", [128, 2])
            rs = sb("rs3", [128, 2])
            hb = sb("hb3", [128, D], BF16)
            hT = [sb(f"hT3_{i}", [128, 8, TT], BF16) for i in range(2)]
            aT = sb("aT", [128, NFC, TT], BF16)
            sg = [sb(f"sg{i}", [128, TT]) for i in range(2)]
            pmix = ps("pmix", [128, 2, 512])
            pT = ps("pT3", [128, D], BF16)
            pgu = [ps(f"pgu{i}", [128, 2, TT]) for i in range(2)]
            pdn = ps("pdn", [128, 2, 512])

            for k in range(8):
                dma("pool", wu[:, k, :], w_up[l, k * 128:(k + 1) * 128, :], writes=[("wu", k)])
            for fc in range(NFC):
                dma("pool", wd[:, fc, :], w_down[l, fc * 128:(fc + 1) * 128, :], writes=[("wd", fc)])
            dma("sp", g_sb[:], gcol[:, 2 * l + 1, :], writes=["g_sb"])
            dma("sp", rf_sb[:], rflag[:, :], writes=["rf_sb"])
            if last:
                dma("sp", gf_sb[:], gfin[:, :], writes=["gf_sb"])
            op("pool", lambda e: e.memset(identf[:], 1.0), writes=["identf"])
            op("pool", lambda e: e.affine_select(out=identf[:], in_=identf[:], pattern=[[-1, 128]],
                                                 compare_op=ALU.is_equal, fill=0.0, base=0,
                                                 channel_multiplier=1), reads=["identf"], writes=["identf"])
            op("dve", lambda e: e.tensor_copy(out=ident[:], in_=identf[:]), reads=["identf"], writes=["ident"])
            op("pool", lambda e: e.memset(mh[:], -0.5), writes=["mh"])

            yv = yT.rearrange("(k p) n -> p k n", p=128)
            def load_x(t):
                sl = t % 2
                dma("sp", xt[sl][:], x_tile256(t), writes=[("xt", sl)])

            def load_y(t):
                sl = t % 2
                for hf, (dstt, dn) in enumerate(((yt[sl], ("yt", sl)), (ytc, "ytc"))):
                    c0 = hf * (S // 2) + t * TT
                    cs3 = slice(c0, c0 + TT)
                    yc, co = divmod(c0, YC)
                    dma("sp", dstt[:, 0:2, :], yv[:, 0:2, cs3], writes=[dn])
                    for gh in range(2 * HL):
                        rk, hh = divmod(gh, HL)
                        dma("sp", dstt[:, 2 + gh, :], ybdst[l][hh][yc][rk * 128:(rk + 1) * 128, co:co + TT], writes=[dn])
                    dma("sp", dstt[:, 6:8, :], yv[:, 2:4, cs3], writes=[dn])

            def blend_y(t):
                sl = t % 2
                op("dve", lambda e: e.tensor_scalar(out=yt[sl][:], in0=yt[sl][:], scalar1=rf_sb[:, 0:1], scalar2=None,
                                                    op0=ALU.mult), reads=[("yt", sl), "rf_sb"], writes=[("yt", sl)])
                op("dve", lambda e: e.scalar_tensor_tensor(out=yt[sl][:], in0=ytc[:], scalar=rf_sb[:, 1:2], in1=yt[sl][:],
                                                           op0=ALU.mult, op1=ALU.add),
                   reads=["ytc", ("yt", sl), "rf_sb"], writes=[("yt", sl)])

            gcnt = [0]

            def A1(t, s):
                sl = t % 2
                for half in range(2):
                    for k in range(8):
                        op("pe", lambda e, k=k, half=half: e.matmul(
                            pmix[:, half, :], lhsT=yt[sl][:, k, s * 128:(s + 1) * 128],
                            rhs=wo[:, k, half * 512:(half + 1) * 512], start=(k == 0), stop=(k == 7)),
                           reads=[("yt", sl), ("wo", k)], writes=["pmix"], signal=(k == 7 and half == 1))
                op("dve", lambda e: e.tensor_tensor(out=xt[sl][:, s, :], in0=xt[sl][:, s, :],
                                                    in1=pmix[:].rearrange("p a b -> p (a b)"), op=ALU.add),
                   reads=[("xt", sl), "pmix"], writes=[("xt", sl)])
                op("act", lambda e: e.activation(out=hb[:], in_=xt[sl][:, s, :], func=AF.Square,
                                                 accum_out=ss[:, s:s + 1]),
                   reads=[("xt", sl)], writes=["hb", ("ss", s)])
                op("dve", lambda e: e.tensor_scalar(out=rs[:, s:s + 1], in0=ss[:, s:s + 1], scalar1=1.0 / D,
                                                    scalar2=RMS_EPS, op0=ALU.mult, op1=ALU.add),
                   reads=[("ss", s)], writes=[("rs", s)])
                op("pool", lambda e: e.tensor_tensor(out=rs[:, s:s + 1], in0=rs[:, s:s + 1], in1=mh[:, 0:1],
                                                     op=ALU.pow), reads=[("rs", s), "mh"], writes=[("rs", s)])
                op("dve", lambda e: e.tensor_scalar(out=hb[:], in0=xt[sl][:, s, :], scalar1=rs[:, s:s + 1],
                                                    scalar2=None, op0=ALU.mult),
                   reads=[("xt", sl), ("rs", s)], writes=["hb"])

            def A2(t, s):
                sl = t % 2
                for k in range(8):
                    op("pe", lambda e, k=k: e.transpose(out=pT[:, k * 128:(k + 1) * 128],
                                                        in_=hb[:, k * 128:(k + 1) * 128], identity=ident[:]),
                       reads=["hb", "ident"], writes=["pT"], signal=(k == 7))
                for k in range(8):
                    op("dve", lambda e, k=k: e.tensor_scalar(
                        out=hT[sl][:, k, s * 128:(s + 1) * 128], in0=pT[:, k * 128:(k + 1) * 128],
                        scalar1=g_sb[:, k:k + 1], scalar2=None, op0=ALU.mult),
                       reads=["pT", "g_sb"], writes=[("hT", sl)])

            def p3_tile(t, hooks):
                sl = t % 2
                for fc in range(NFC):
                    gs = gcnt[0] % 2
                    gcnt[0] += 1
                    for (a, wt_, wn) in ((0, wg, "wg"), (1, wu, "wu")):
                        for k in range(8):
                            op("pe", lambda e, k=k, a=a, wt_=wt_, fc=fc, gs=gs: e.matmul(
                                pgu[gs][:, a, :], lhsT=wt_[:, k, fc * 128:(fc + 1) * 128], rhs=hT[sl][:, k, :],
                                start=(k == 0), stop=(k == 7)),
                               reads=[("hT", sl), (wn, k)], writes=[("pgu", gs)], signal=(k == 7 and a == 1))
                    op("act", lambda e, gs=gs: e.activation(out=sg[gs][:], in_=pgu[gs][:, 0, :], func=AF.Silu),
                       reads=[("pgu", gs)], writes=[("sg", gs)])
                    op("dve", lambda e, gs=gs, fc=fc: e.tensor_tensor(out=aT[:, fc, :], in0=pgu[gs][:, 1, :],
                                                                      in1=sg[gs][:], op=ALU.mult),
                       reads=[("pgu", gs), ("sg", gs)], writes=["aT"])
                    for f in hooks.get(fc, []):
                        f()
                for s in range(2):
                    for half in range(2):
                        for fc in range(NFC):
                            op("pe", lambda e, fc=fc, s=s, half=half: e.matmul(
                                pdn[:, half, :], lhsT=aT[:, fc, s * 128:(s + 1) * 128],
                                rhs=wd[:, fc, half * 512:(half + 1) * 512], start=(fc == 0), stop=(fc == NFC - 1)),
                               reads=["aT", ("wd", fc)], writes=["pdn"], signal=(fc == NFC - 1 and half == 1))
                    op("dve", lambda e, s=s: e.tensor_tensor(out=xt[sl][:, s, :], in0=xt[sl][:, s, :],
                                                             in1=pdn[:].rearrange("p a b -> p (a b)"), op=ALU.add),
                       reads=[("xt", sl), "pdn"], writes=[("xt", sl)])
                    if last:
                        op("act", lambda e, s=s: e.activation(out=hb[:], in_=xt[sl][:, s, :], func=AF.Square,
                                                              accum_out=ss[:, s:s + 1]),
                           reads=[("xt", sl)], writes=["hb", ("ss", s)])
                        op("dve", lambda e, s=s: e.tensor_scalar(out=rs[:, s:s + 1], in0=ss[:, s:s + 1], scalar1=1.0 / D,
                                                                 scalar2=RMS_EPS, op0=ALU.mult, op1=ALU.add),
                           reads=[("ss", s)], writes=[("rs", s)])
                        op("pool", lambda e, s=s: e.tensor_tensor(out=rs[:, s:s + 1], in0=rs[:, s:s + 1], in1=mh[:, 0:1],
                                                                  op=ALU.pow), reads=[("rs", s), "mh"], writes=[("rs", s)])
                        op("dve", lambda e, s=s: e.scalar_tensor_tensor(out=xt[sl][:, s, :], in0=xt[sl][:, s, :],
                                                                        scalar=rs[:, s:s + 1], in1=gf_sb[:],
                                                                        op0=ALU.mult, op1=ALU.mult),
                           reads=[("xt", sl), ("rs", s), "gf_sb"], writes=[("xt", sl)])
                tk = dma("sp", x_tile256(t, store=True), xt[sl][:], reads=[("xt", sl)], writes=["xo"])
                if not last:
                    x_toks.append(tk)
                    per = XCH // 256
                    if (t + 1) % per == 0:
                        c = t // per
                        sch.collective(x_toks[-per:], lambda e, c=c: e.collective_compute(
                            "AllGather", ALU.bypass, replica_groups=PAIRS,
                            ins=[xsrc[l][c][:, :]], outs=[xg[l][c][:, :]]), cc_semx)

            x_toks = []
            cc_semx = nc.alloc_semaphore(f"cc_semx{l}")
            load_x(0)
            load_y(0)
            blend_y(0)
            for s in range(2):
                A1(0, s)
                A2(0, s)
            if NT3 > 1:
                load_y(1)
            for t in range(NT3):
                hooks = {}
                if t + 1 < NT3:
                    load_x(t + 1)
                    hooks[3] = [lambda t=t: blend_y(t + 1)]
                    hooks[7] = [lambda t=t: A1(t + 1, 0)]
                    hooks[10] = [lambda t=t: A2(t + 1, 0)]
                    hooks[13] = [lambda t=t: A1(t + 1, 1)]
                    hooks[16] = [lambda t=t: A2(t + 1, 1)]
                if t + 2 < NT3:
                    hooks.setdefault(8, []).append(lambda t=t: load_y(t + 2))
                p3_tile(t, hooks)
            if not last:
                sch.pool_wait(cc_semx, NXC)
                op("pool", lambda e: e.memset(mh[:], -0.5), writes=["mh"])
            sch.end_phase()
        es_w.close()

    return nc


def _prep(inputs, S):
    f32 = np.float32
    depth = inputs["w_in"].shape[0]
    offs = np.cumsum([0, 256, 256, 256, 512, 512, 512, 512])
    a0, q0, k0, v0, c0 = 0, offs[3], offs[4], offs[5], offs[6]
    idx = list(range(a0, a0 + 768)) + list(range(c0, c0 + 512))

    def head_cols(base, heads):
        cols = []
        for h in heads:
            for mm in range(2):
                st = base + mm * 256 + h * 64
                cols += list(range(st, st + 64))
        return cols
    w_in_rank = []
    for r in range(2):
        heads = [HL * r + i for i in range(HL)]
        idr = list(idx) + head_cols(q0, heads) + head_cols(k0, heads)
        for h in heads:
            idr += list(range(v0 + h * 128, v0 + (h + 1) * 128))
        idr = np.asarray(idr)
        assert idr.size == WIN_EXT
        w_in_rank.append(np.ascontiguousarray(np.asarray(inputs["w_in"], f32)[:, :, idr]))
    w_in_ext = w_in_rank[0]

    def cols128(v):
        v = np.asarray(v, f32)
        return np.ascontiguousarray(v.reshape(-1, 128).T)

    gcol = np.stack([cols128(inputs["mix_norm_g"][l]) if i == 0 else cols128(inputs["ffn_norm_g"][l])
                     for l in range(depth) for i in range(2)], axis=1)
    gfin = np.ascontiguousarray(np.broadcast_to(np.asarray(inputs["final_norm_g"], f32)[None, :], (128, D)))
    scw = np.stack([np.concatenate([np.stack([np.asarray(inputs["short_conv_w"], f32)[l, j, c * 128:(c + 1) * 128]
                                              for j in range(3)], axis=1) for c in range(2)], axis=1)
                    for l in range(depth)])
    glub = np.stack([cols128(inputs["glu_b"][l]) for l in range(depth)])
    cdw = np.stack([np.concatenate([np.stack([np.asarray(inputs["conf_dw_w"], f32)[l, j, c * 128:(c + 1) * 128]
                                              for j in range(31)], axis=1) for c in range(2)], axis=1)
                    for l in range(depth)])
    cvec = np.stack([np.concatenate([cols128(inputs["conf_dw_b"][l]), cols128(inputs["conf_ln_g"][l]),
                                     cols128(inputs["conf_ln_b"][l])], axis=1) for l in range(depth)])
    lamv = np.stack([np.broadcast_to(np.concatenate([np.asarray(inputs[k], f32)[l] for k in
                                                     ("lam_q1", "lam_k1", "lam_q2", "lam_k2")])[None, :], (128, 256))
                     for l in range(depth)])
    subg = np.stack([np.asarray(inputs["diff_subln_g"], f32)[l].reshape(128, 1) for l in range(depth)])
    inv_freq = (1.0 / (np.float32(10000.0) ** (np.arange(0, 64, 2, dtype=np.float32) / np.float32(64)))).astype(f32)
    p = np.arange(128)
    cst = np.zeros((128, 4), f32)
    cst[:, 0] = inv_freq[p % 32]
    cst[:, 1] = np.where((p % 64) < 32, -1.0, 1.0)
    gtok = np.ascontiguousarray(np.broadcast_to(np.asarray(inputs["mix_norm_g"], f32)[:, None, :], (depth, 128, D)))
    shared = dict(cst=cst, gcol=np.ascontiguousarray(gcol), gfin=gfin, gtok=gtok, w_in=w_in_ext,
                  w_out=np.ascontiguousarray(np.asarray(inputs["w_out"], f32)),
                  w_gate=np.ascontiguousarray(np.asarray(inputs["w_gate"], f32)),
                  w_up=np.ascontiguousarray(np.asarray(inputs["w_up"], f32)),
                  w_down=np.ascontiguousarray(np.asarray(inputs["w_down"], f32)),
                  scw=np.ascontiguousarray(scw), glub=np.ascontiguousarray(glub), cdw=np.ascontiguousarray(cdw),
                  cvec=np.ascontiguousarray(cvec), lamv=np.ascontiguousarray(lamv), subg=np.ascontiguousarray(subg))
    x = np.asarray(inputs["x"], f32)
    pos = np.asarray(inputs["positions"]).astype(np.int32)
    maps = []
    for b in range(x.shape[0]):
        xb = np.ascontiguousarray(x[b])
        pb_ = np.ascontiguousarray(np.broadcast_to(pos[b][None, :], (128, S)))
        for r in range(2):
            mp = dict(shared)
            mp["w_in"] = w_in_rank[r]
            mp["x"] = xb
            mp["xh"] = np.ascontiguousarray(xb[r * (S // 2):(r + 1) * (S // 2)])
            fl = np.zeros((128, 2), f32)
            fl[:, r] = 1.0
            mp["rflag"] = fl
            mp["pos"] = pb_
            maps.append(mp)
    return maps


_NC_CACHE = {}


def kernel(**inputs):
    x = np.asarray(inputs["x"])
    B, S, _ = x.shape
    depth = inputs["w_in"].shape[0]
    maps = _prep(inputs, S)
    n_cores = 8
    in_maps = [maps[c % (2 * B)] for c in range(n_cores)]
    key = (S, depth)
    if key not in _NC_CACHE:
        _NC_CACHE[key] = build_nc(S, depth)
    nc = _NC_CACHE[key]
    res = run_bass_kernel_spmd(nc, in_maps, core_ids=list(range(n_cores)))
    return np.stack([np.concatenate([np.asarray(res.results[2 * b + r]["out"], np.float32) for r in range(2)], axis=0)
                     for b in range(B)], axis=0)
```

```python
import math
from contextlib import ExitStack

import numpy as np
import concourse.bass as bass
import concourse.mybir as mybir
from concourse.bass_utils import run_bass_kernel_spmd

F32 = mybir.dt.float32
BF16 = mybir.dt.bfloat16
I32 = mybir.dt.int32
ALU = mybir.AluOpType
AF = mybir.ActivationFunctionType

D = 1024
DEPTH = 2
FF = 2816
NFC = FF // 128
HL = 2
WIN_EXT = 1664
G_AB, G_AC, G_AX, G_CV, G_CG, G_Q, G_K = 0, 1, 2, 3, 5, 7, 9
V_OFF = 11 * 128
PAIRS = [[0, 1], [2, 3], [4, 5], [6, 7]]
RMS_EPS = 1e-6
LN_EPS = 1e-5
TWO_PI = 2.0 * math.pi
C1 = 6.28125
C2 = TWO_PI - C1


class Sched:
    ENGS = ("pe", "act", "dve", "pool", "sp")

    def __init__(self, nc, n_dma_sems=32, same_engine_sync=True):
        self.nc = nc
        self.same_engine_sync = same_engine_sync
        self.n_dma_sems = n_dma_sems
        self.dma_sems = [nc.alloc_semaphore(f"dq{i}") for i in range(n_dma_sems)]
        self.dma_cnt = [0] * n_dma_sems
        self.dma_last = [None] * n_dma_sems
        self.dma_i = 0
        self.dma_i_sw = 0
        self.phase = -1
        self.res_w = {}
        self.res_r = {}
        self.n_ops = 0
        self.new_phase()

    def new_phase(self):
        self.phase += 1
        self.prog = {e: [] for e in self.ENGS}
        self.esem = {e: self.nc.alloc_semaphore(f"s_{e}_{self.phase}") for e in self.ENGS}
        self.seq = {e: 0 for e in self.ENGS}
        self.sig = {e: 0 for e in self.ENGS}
        self.seq2sig = {e: [] for e in self.ENGS}
        self.pending = {e: [] for e in self.ENGS}
        self.seen = {e: {} for e in self.ENGS}
        for e in self.ENGS:
            for j in range(self.n_dma_sems):
                self.seen[e][("d", j)] = 16 * self.dma_cnt[j]

    def _resolve(self, tok):
        if tok[0] == "d":
            return (("d", tok[1]), self.dma_sems[tok[1]], tok[2])
        _, eng, ph, seq = tok
        v = self.seq2sig[eng][seq]
        assert v is not None, f"dependency on unsignaled instruction {tok}"
        return (("e", eng), self.esem[eng], v)

    def _deps(self, eng, reads, writes):
        toks = []
        for r in reads:
            t = self.res_w.get(r)
            if t is not None:
                toks.append(t)
        for w in writes:
            t = self.res_w.get(w)
            if t is not None:
                toks.append(t)
            toks.extend(self.res_r.get(w, []))
        waits = {}
        for t in toks:
            if t[0] == "e":
                if t[2] != self.phase:
                    continue
                if t[1] == eng and (not self.same_engine_sync or self.seq2sig[eng][t[3]] is None):
                    continue
            key, sem, v = self._resolve(t)
            if self.seen[eng].get(key, 0) >= v:
                continue
            if key not in waits or waits[key][1] < v:
                waits[key] = (sem, v)
        for key, (sem, v) in waits.items():
            self.seen[eng][key] = v
        return list(waits.values())

    def _record(self, tok, reads, writes):
        for r in reads:
            self.res_r.setdefault(r, []).append(tok)
        for w in writes:
            self.res_w[w] = tok
            self.res_r[w] = []

    def op(self, eng, fn, reads=(), writes=(), signal=True):
        waits = self._deps(eng, reads, writes)
        seq = self.seq[eng]
        self.seq[eng] += 1
        tok = ("e", eng, self.phase, seq)
        self.seq2sig[eng].append(None)
        self.pending[eng].append(seq)
        if signal:
            self.sig[eng] += 1
            for s in self.pending[eng]:
                self.seq2sig[eng][s] = self.sig[eng]
            self.pending[eng] = []
        sem = self.esem[eng]

        def thunk(e, fn=fn, waits=waits, signal=signal, sem=sem):
            for (s, v) in waits:
                e.wait_ge(s, v)
            ins = fn(e)
            if signal:
                ins.then_inc(sem, 1)
        self.prog[eng].append(thunk)
        self._record(tok, reads, writes)
        self.n_ops += 1
        return tok

    def dma(self, eng, out, in_, reads=(), writes=(), **kw):
        n_sw = 8
        if eng == "pool":
            j = self.dma_i_sw % n_sw
            self.dma_i_sw += 1
        else:
            j = n_sw + self.dma_i % (self.n_dma_sems - n_sw)
            self.dma_i += 1
        waits = self._deps(eng, reads, writes)
        prev = self.dma_last[j]
        if prev is not None:
            key, sem, v = self._resolve(prev)
            if self.seen[eng].get(key, 0) < v:
                waits.append((sem, v))
                self.seen[eng][key] = v
        self.dma_cnt[j] += 1
        val = 16 * self.dma_cnt[j]
        tok = ("d", j, val)
        self.dma_last[j] = tok
        sem = self.dma_sems[j]

        def thunk(e, waits=waits, sem=sem, out=out, in_=in_, kw=kw):
            for (s, v) in waits:
                e.wait_ge(s, v)
            e.dma_start(out=out, in_=in_, **kw).then_inc(sem, 16)
        self.prog[eng].append(thunk)
        self._record(tok, reads, writes)
        self.n_ops += 1
        return tok

    def collective(self, wait_toks, emit_fn, cc_sem):
        waits = [self._resolve(t)[1:] for t in wait_toks]

        def thunk(e, waits=waits, emit_fn=emit_fn, cc_sem=cc_sem):
            for (s_, v) in waits:
                e.wait_ge(s_, v)
            emit_fn(e).then_inc(cc_sem)
        self.prog["pool"].append(thunk)

    def pool_wait(self, sem, val):
        def thunk(e, sem=sem, val=val):
            e.wait_ge(sem, val)
        self.prog["pool"].append(thunk)

    def end_phase(self):
        for e in self.ENGS:
            assert not self.pending[e], f"unsignaled trailing instructions on {e}"
        targets = []
        for e in self.ENGS:
            if self.sig[e] > 0:
                targets.append((self.esem[e], self.sig[e]))
        for j in range(self.n_dma_sems):
            if self.dma_cnt[j] > 0:
                targets.append((self.dma_sems[j], 16 * self.dma_cnt[j]))
        prog = self.prog
        nc = self.nc

        def run(e, name):
            for t in prog[name]:
                t(e)
            for (s, v) in targets:
                e.wait_ge(s, v)

        with nc.Block() as block:
            @block.tensor
            def _(e):
                run(e, "pe")

            @block.scalar
            def _(e):
                run(e, "act")

            @block.vector
            def _(e):
                run(e, "dve")

            @block.gpsimd
            def _(e):
                run(e, "pool")

            @block.sync
            def _(e):
                run(e, "sp")
        self.res_w = {}
        self.res_r = {}
        self.dma_last = [None] * self.n_dma_sems
        self.new_phase()


def build_nc(S, depth=DEPTH):
    NT = S // 512
    NKB = S // 128
    nc = bass.Bass("TRN2", target_bir_lowering=False)

    def din(name, shape, dt=F32):
        return nc.dram_tensor(name, list(shape), dt, kind="ExternalInput").ap()

    def dscr(name, shape, dt):
        return nc.dram_tensor(name, list(shape), dt, kind="Internal").ap()

    x_in = din("x", [S, D])
    xh_in = din("xh", [S // 2, D])
    rflag = din("rflag", [128, 2])
    pos_in = din("pos", [128, S], I32)
    cst = din("cst", [128, 4])
    gcol = din("gcol", [128, 2 * depth, 8])
    gfin = din("gfin", [128, D])
    gtok = din("gtok", [depth, 128, D])
    w_in = din("w_in", [depth, D, WIN_EXT])
    w_out = din("w_out", [depth, D, D])
    w_gate = din("w_gate", [depth, D, FF])
    w_up = din("w_up", [depth, D, FF])
    w_down = din("w_down", [depth, FF, D])
    scw = din("scw", [depth, 128, 3])
    glub = din("glub", [depth, 128, 4])
    cdw = din("cdw", [depth, 128, 2 * 31])
    cvec = din("cvec", [depth, 128, 6])
    lamv = din("lamv", [depth, 128, 4 * 64])
    subg = din("subg", [depth, 128, 1])
    out = nc.dram_tensor("out", [S // 2, D], F32, kind="ExternalOutput").ap()
    XCH = 512
    yasrc = [nc.dram_tensor(f"yasrc{i}", [128, S], BF16) for i in range(depth)]
    yadst = [nc.dram_tensor(f"yadst{i}", [256, S], BF16) for i in range(depth)]
    NXC = (S // 2) // XCH
    xsrc = [[nc.dram_tensor(f"xsrc{i}_{c}", [XCH, D], F32) for c in range(NXC)] for i in range(depth - 1)]
    xg = [[nc.dram_tensor(f"xg{i}_{c}", [2 * XCH, D], F32) for c in range(NXC)] for i in range(depth - 1)]

    qT = dscr("qT", [HL, 128, S], BF16)
    kT = dscr("kT", [HL, 128, S], BF16)
    vS = dscr("vS", [S, HL * 128], BF16)
    ybsrc = [[nc.dram_tensor(f"ybsrc{i}_{h}", [128, S], BF16) for h in range(HL)] for i in range(depth)]
    ybdst = [[nc.dram_tensor(f"ybdst{i}_{h}", [2 * 128, S], BF16) for h in range(HL)] for i in range(depth)]
    yT = dscr("yT", [512, S], BF16)
    cosT = dscr("cosT", [128, S], F32)
    sinT = dscr("sinT", [128, S], F32)

    sch = Sched(nc)
    op, dma = sch.op, sch.dma

    CH = min(S, 2048)
    with ExitStack() as es:
        def sb(name, shape, dt=F32):
            return es.enter_context(nc.sbuf_tensor(name, list(shape), dt))
        c_sb = sb("c_sb", [128, 4])
        posi = sb("posi", [128, CH], I32)
        ang = sb("ang", [128, CH])
        kf = sb("kf", [128, CH])
        ki = sb("ki", [128, CH], I32)
        r = sb("r", [128, CH])
        m = sb("m", [128, CH])
        rc = sb("rc", [128, CH])
        so = sb("so", [128, CH])
        co = sb("co", [128, CH])
        dma("sp", c_sb[:], cst[:, :], writes=["c_sb"])
        for c in range(S // CH):
            cs = slice(c * CH, (c + 1) * CH)
            dma("sp", posi[:], pos_in[:, cs], writes=["posi"])
            op("dve", lambda e: e.tensor_copy(out=ang[:], in_=posi[:]), reads=["posi"], writes=["ang"])
            op("dve", lambda e: e.tensor_scalar(out=ang[:], in0=ang[:], scalar1=c_sb[:, 0:1], scalar2=None,
                                                op0=ALU.mult), reads=["ang", "c_sb"], writes=["ang"])
            op("dve", lambda e: e.tensor_scalar(out=kf[:], in0=ang[:], scalar1=1.0 / TWO_PI, scalar2=None,
                                                op0=ALU.mult), reads=["ang"], writes=["kf"])
            op("dve", lambda e: e.tensor_copy(out=ki[:], in_=kf[:]), reads=["kf"], writes=["ki"])
            op("dve", lambda e: e.tensor_copy(out=kf[:], in_=ki[:]), reads=["ki"], writes=["kf"])
            op("dve", lambda e: e.scalar_tensor_tensor(out=r[:], in0=kf[:], scalar=-C1, in1=ang[:],
                                                       op0=ALU.mult, op1=ALU.add), reads=["kf", "ang"], writes=["r"])
            op("dve", lambda e: e.scalar_tensor_tensor(out=r[:], in0=kf[:], scalar=-C2, in1=r[:],
                                                       op0=ALU.mult, op1=ALU.add), reads=["kf", "r"], writes=["r"])

            def fold(dst, dn):
                op("dve", lambda e: e.tensor_scalar(out=m[:], in0=dst[:], scalar1=math.pi, scalar2=TWO_PI,
                                                    op0=ALU.is_gt, op1=ALU.mult), reads=[dn], writes=["m"])
                op("dve", lambda e: e.tensor_tensor(out=dst[:], in0=dst[:], in1=m[:], op=ALU.subtract),
                   reads=[dn, "m"], writes=[dn])
                op("dve", lambda e: e.tensor_scalar(out=dst[:], in0=dst[:], scalar1=-math.pi, scalar2=math.pi,
                                                    op0=ALU.max, op1=ALU.min), reads=[dn], writes=[dn])
            fold(r, "r")
            op("dve", lambda e: e.tensor_scalar(out=rc[:], in0=r[:], scalar1=math.pi / 2, scalar2=None,
                                                op0=ALU.add), reads=["r"], writes=["rc"])
            fold(rc, "rc")
            op("dve", lambda e: e.tensor_scalar(out=r[:], in0=r[:], scalar1=c_sb[:, 1:2], scalar2=None,
                                                op0=ALU.mult), reads=["r", "c_sb"], writes=["r"])
            op("act", lambda e: e.activation(out=so[:], in_=r[:], func=AF.Sin), reads=["r"], writes=["so"])
            op("act", lambda e: e.activation(out=co[:], in_=rc[:], func=AF.Sin), reads=["rc"], writes=["co"])
            dma("sp", sinT[:, cs], so[:], reads=["so"], writes=["sinT"])
            dma("sp", cosT[:, cs], co[:], reads=["co"], writes=["cosT"])
        sch.end_phase()

    for l in range(depth):
        def x_tile512(t, l=l):
            if l == 0:
                return x_in.rearrange("(t s p) d -> t p s d", p=128, s=4)[t]
            part, c = divmod(t, NXC)
            return xg[l - 1][c][part * XCH:(part + 1) * XCH, :].rearrange("(s p) d -> p s d", p=128)

        def x_tile256(t, l=l, store=False):
            if store:
                if l == depth - 1:
                    return out.rearrange("(t s p) d -> t p s d", p=128, s=2)[t]
                c, o = divmod(t, XCH // 256)
                return xsrc[l][c][o * 256:(o + 1) * 256, :].rearrange("(s p) d -> p s d", p=128)
            if l == 0:
                return xh_in.rearrange("(t s p) d -> t p s d", p=128, s=2)[t]
            c, o = divmod(t, XCH // 256)
            return xsrc[l - 1][c][o * 256:(o + 1) * 256, :].rearrange("(s p) d -> p s d", p=128)
        lam_init = 0.8 - 0.6 * math.exp(-0.3 * l)
        last = (l == depth - 1)

        with ExitStack() as es:
            def sb(name, shape, dt=F32):
                return es.enter_context(nc.sbuf_tensor(f"{name}_L{l}", list(shape), dt))

            def ps(name, shape, dt=F32):
                return es.enter_context(nc.psum_tensor(f"{name}_L{l}", list(shape), dt))
            wsb = sb("wsb", [128, 8, WIN_EXT], BF16)
            g_sb = sb("g_sb", [128, 8])
            scw_sb = sb("scw_sb", [128, 3])
            glub_sb = sb("glub_sb", [128, 4])
            cdw_sb = sb("cdw_sb", [128, 62])
            cvec_sb = sb("cvec_sb", [128, 6])
            identf = sb("identf", [128, 128])
            ident = sb("ident", [128, 128], BF16)
            diag = sb("diag", [128, 62, 128], BF16)
            onesM = sb("onesM", [128, 128])
            pswap = sb("pswap", [128, 128], BF16)
            qb = [sb(f"qb{i}", [128, 512], BF16) for i in range(2)]
            mh = sb("mh", [128, 512])
            xt = [sb(f"xt{i}", [128, 4, D]) for i in range(2)]
            ss = [sb(f"ss{i}", [128, 4]) for i in range(2)]
            rs = [sb(f"rs{i}", [128, 4]) for i in range(2)]
            hb = [sb(f"hb{i}", [128, D], BF16) for i in range(2)]
            gt_sb = sb("gt_sb", [128, D])
            junk = sb("junk", [128, D], BF16)
            hT = [sb(f"hT{i}", [128, 8, 512], BF16) for i in range(2)]
            cs_sb = [sb(f"cos{i}", [128, 512]) for i in range(2)]
            sn_sb = [sb(f"sin{i}", [128, 512]) for i in range(2)]
            zc = [sb(f"zc{c}", [128, 2 + 512]) for c in range(2)]
            cbuf = [sb(f"cbuf{c}", [128, 32 + 512], BF16) for c in range(2)]
            tA = sb("tA", [128, 512])
            tB = sb("tB", [128, 512])
            tS = sb("tS", [128, 512])
            t1 = [sb(f"t1_{i}", [128, 512]) for i in range(2)]
            t2 = [sb(f"t2_{i}", [128, 512]) for i in range(2)]
            cv = [[sb(f"cv{i}_{c}", [128, 512]) for c in range(2)] for i in range(2)]
            sq = [sb(f"sq{c}", [128, 512]) for c in range(2)]
            rstd = sb("rstd", [128, 512])
            ob = [sb(f"ob{i}", [128, 512], BF16) for i in range(4)]
            vb = [sb(f"vb{i}", [128, HL * 128], BF16) for i in range(2)]
            pT = [ps(f"pT{i}", [128, D], BF16) for i in range(2)]
            pp = [ps(f"pp{i}", [128, 512]) for i in range(4)]
            pcv = ps("pcv", [128, 512])
            pst = ps("pst", [128, 512])

            for k in range(8):
                dma("pool", wsb[:, k, :], w_in[l, k * 128:(k + 1) * 128, :], writes=[("wsb", k)])
            dma("sp", gt_sb[:], gtok[l], writes=["gt_sb"])
            dma("sp", scw_sb[:], scw[l], writes=["scw_sb"])
            dma("sp", glub_sb[:], glub[l], writes=["glub_sb"])
            dma("sp", cdw_sb[:], cdw[l], writes=["cdw_sb"])
            dma("sp", cvec_sb[:], cvec[l], writes=["cvec_sb"])
            op("pool", lambda e: e.memset(identf[:], 1.0), writes=["identf"])
            op("pool", lambda e: e.affine_select(out=identf[:], in_=identf[:], pattern=[[-1, 128]],
                                                 compare_op=ALU.is_equal, fill=0.0, base=0,
                                                 channel_multiplier=1), reads=["identf"], writes=["identf"])
            op("dve", lambda e: e.tensor_copy(out=ident[:], in_=identf[:]), reads=["identf"], writes=["ident"])
            op("pool", lambda e: e.memset(onesM[:], 1.0 / 256.0), writes=["onesM"])
            op("pool", lambda e: e.memset(mh[:], -0.5), writes=["mh"])
            for b64 in range(2):
                for hh in range(2):
                    d0 = 64 * b64 + 32 * hh
                    s0 = 64 * b64 + 32 * (1 - hh)
                    op("dve", lambda e, d0=d0, s0=s0: e.tensor_copy(out=pswap[:, d0:d0 + 32], in_=identf[:, s0:s0 + 32]),
                       reads=["identf"], writes=["pswap"])
            for c in range(2):
                op("pool", lambda e, c=c: e.memset(zc[c][:, 0:2], 0.0), writes=[("zc", c)])
                op("pool", lambda e, c=c: e.memset(cbuf[c][:, 0:32], 0.0), writes=[("cbuf", c)])
            for i in range(62):
                op("dve", lambda e, i=i: e.tensor_scalar(out=diag[:, i, :], in0=identf[:], scalar1=cdw_sb[:, i:i + 1],
                                                         scalar2=None, op0=ALU.mult),
                   reads=["identf", "cdw_sb"], writes=["diag"])

            pp_i = [0]
            ob_i = [0]

            def ab_load(t, x=True, cs=True):
                sl = t % 2
                if x:
                    dma("sp", xt[sl][:], x_tile512(t), writes=[("xt", sl)])
                if cs:
                    dma("sp", cs_sb[sl][:], cosT[:, t * 512:(t + 1) * 512], writes=[("cos", sl)])
                    dma("sp", sn_sb[sl][:], sinT[:, t * 512:(t + 1) * 512], writes=[("sin", sl)])

            def ab_stats(t):
                sl = t % 2
                for s in range(4):
                    op("act", lambda e, s=s: e.activation(out=junk[:], in_=xt[sl][:, s, :], func=AF.Square,
                                                          accum_out=ss[sl][:, s:s + 1]),
                       reads=[("xt", sl)], writes=["junk", ("ss", sl)])
                op("dve", lambda e: e.tensor_scalar(out=rs[sl][:], in0=ss[sl][:], scalar1=1.0 / D, scalar2=RMS_EPS,
                                                    op0=ALU.mult, op1=ALU.add),
                   reads=[("ss", sl)], writes=[("rs", sl)])
                op("pool", lambda e: e.tensor_tensor(out=rs[sl][:], in0=rs[sl][:], in1=mh[:, 0:4], op=ALU.pow),
                   reads=[("rs", sl), "mh"], writes=[("rs", sl)])

            def ab_sub_a(t, s):
                sl = t % 2
                b_ = s % 2
                op("dve", lambda e: e.scalar_tensor_tensor(out=hb[b_][:], in0=xt[sl][:, s, :], scalar=rs[sl][:, s:s + 1],
                                                           in1=gt_sb[:], op0=ALU.mult, op1=ALU.mult),
                   reads=[("xt", sl), ("rs", sl), "gt_sb"], writes=[("hb", b_)])

            def ab_sub(t, s, with_a=True):
                sl = t % 2
                b_ = s % 2
                if with_a:
                    ab_sub_a(t, s)
                for k in range(8):
                    op("pe", lambda e, k=k: e.transpose(out=pT[b_][:, k * 128:(k + 1) * 128],
                                                        in_=hb[b_][:, k * 128:(k + 1) * 128], identity=ident[:]),
                       reads=[("hb", b_), "ident"], writes=[("pT", b_)], signal=(k == 7))
                op("act", lambda e: e.copy(out=hT[sl][:, :, s * 128:(s + 1) * 128],
                                           in_=pT[b_][:].rearrange("p (k c) -> p k c", k=8)),
                   reads=[("pT", b_)], writes=[("hT", sl)])

            def proj(t, gi):
                sl = t % 2
                i = pp_i[0] % 4
                pp_i[0] += 1
                for k in range(8):
                    op("pe", lambda e, k=k, i=i: e.matmul(pp[i][:], lhsT=wsb[:, k, gi * 128:(gi + 1) * 128],
                                                          rhs=hT[sl][:, k, :], start=(k == 0), stop=(k == 7)),
                       reads=[("hT", sl), ("wsb", k)], writes=[("pp", i)], signal=(k == 7))
                return pp[i], ("pp", i)

            def c1(t, hooks):
                sl = t % 2
                cols = slice(t * 512, (t + 1) * 512)
                gcount = [0]

                def hook():
                    for f in hooks.get(gcount[0], []):
                        f()
                    gcount[0] += 1
                for c in range(2):
                    pg, rg = proj(t, G_CG + c)
                    op("act", lambda e, pg=pg, c=c: e.activation(out=tS[:], in_=pg[:], func=AF.Sigmoid,
                                                                 bias=glub_sb[:, 2 + c:3 + c]),
                       reads=[rg, "glub_sb"], writes=["tS"])
                    hook()
                    pvl, rv = proj(t, G_CV + c)
                    op("dve", lambda e, pvl=pvl, c=c: e.scalar_tensor_tensor(
                        out=cbuf[c][:, 32:544], in0=pvl[:], scalar=glub_sb[:, c:c + 1], in1=tS[:],
                        op0=ALU.add, op1=ALU.mult), reads=[rv, "tS", "glub_sb"], writes=[("cbuf", c)])
                    hook()
                for c in range(1):
                    pc, rcn = proj(t, G_AC + c)
                    op("act", lambda e, pc=pc: e.copy(out=tA[:], in_=pc[:]), reads=[rcn], writes=["tA"])
                    hook()
                    px, rx = proj(t, G_AX + c)
                    op("dve", lambda e, px=px, c=c: e.tensor_tensor(out=zc[c][:, 2:514], in0=px[:], in1=tA[:],
                                                                    op=ALU.mult),
                       reads=[rx, "tA"], writes=[("zc", c)])
                    op("dve", lambda e, c=c: e.tensor_scalar(out=tB[:], in0=zc[c][:, 0:512],
                                                             scalar1=scw_sb[:, 3 * c:3 * c + 1], scalar2=None,
                                                             op0=ALU.mult),
                       reads=[("zc", c), "scw_sb"], writes=["tB"])
                    for j in (1, 2):
                        op("dve", lambda e, c=c, j=j: e.scalar_tensor_tensor(
                            out=tB[:], in0=zc[c][:, j:j + 512], scalar=scw_sb[:, 3 * c + j:3 * c + j + 1],
                            in1=tB[:], op0=ALU.mult, op1=ALU.add),
                           reads=[("zc", c), "scw_sb", "tB"], writes=["tB"])
                    op("dve", lambda e, c=c: e.tensor_copy(out=zc[c][:, 0:2], in_=zc[c][:, 512:514]),
                       reads=[("zc", c)], writes=[("zc", c)])
                    hook()
                    pb_, rb = proj(t, G_AB + c)
                    o = ob_i[0] % 4
                    ob_i[0] += 1
                    op("dve", lambda e, pb_=pb_, o=o: e.tensor_tensor(out=ob[o][:], in0=pb_[:], in1=tB[:], op=ALU.mult),
                       reads=[rb, "tB"], writes=[("ob", o)])
                    tk = dma("sp", yasrc[l][:, cols], ob[o][:], reads=[("ob", o)], writes=["yasrc"])
                    ya_toks.append(tk)
                    for _ in range(4):
                        hook()
                pend = [None]

                def rope_tail(pq, rq, h, i2, dst, qi):
                    def f():
                        i = pp_i[0] % 4
                        pp_i[0] += 1
                        op("pe", lambda e, i=i: e.matmul(pp[i][:], lhsT=pswap[:], rhs=qb[qi][:], start=True, stop=True),
                           reads=["pswap", ("qb", qi)], writes=[("pp", i)])
                        op("dve", lambda e, i=i: e.tensor_tensor(out=t2[i2][:], in0=pp[i][:], in1=sn_sb[sl][:], op=ALU.mult),
                           reads=[("pp", i), ("sin", sl)], writes=[("t2", i2)])
                        o = ob_i[0] % 4
                        ob_i[0] += 1
                        op("pool", lambda e, o=o: e.tensor_tensor(out=ob[o][:], in0=t1[i2][:], in1=t2[i2][:], op=ALU.add),
                           reads=[("t1", i2), ("t2", i2)], writes=[("ob", o)])
                        dma("sp", dst[h, :, cols], ob[o][:], reads=[("ob", o)], writes=[dst.name])
                    return f
                qcnt = 0
                for (g0, dst) in ((G_Q, qT), (G_K, kT)):
                    for h in range(HL):
                        i2 = qcnt % 2
                        qi = qcnt % 2
                        qcnt += 1
                        pq, rq = proj(t, g0 + h)
                        op("dve", lambda e, pq=pq, qi=qi: e.tensor_copy(out=qb[qi][:], in_=pq[:]), reads=[rq],
                           writes=[("qb", qi)])
                        op("dve", lambda e, pq=pq, i2=i2: e.tensor_tensor(out=t1[i2][:], in0=pq[:], in1=cs_sb[sl][:],
                                                                          op=ALU.mult),
                           reads=[rq, ("cos", sl), ("qb", qi)], writes=[("t1", i2)])
                        if pend[0] is not None:
                            pend[0]()
                        pend[0] = rope_tail(pq, rq, h, i2, dst, qi)
                        for _ in range(4):
                            hook()
                pend[0]()
                for s in range(4):
                    i = pp_i[0] % 4
                    pp_i[0] += 1
                    for k in range(8):
                        op("pe", lambda e, k=k, s=s, i=i: e.matmul(pp[i][:, 0:HL * 128], lhsT=hT[sl][:, k, s * 128:(s + 1) * 128],
                                                                   rhs=wsb[:, k, V_OFF:V_OFF + HL * 128],
                                                                   start=(k == 0), stop=(k == 7)),
                           reads=[("hT", sl), ("wsb", k)], writes=[("pp", i)], signal=(k == 7))
                    vi = s % 2
                    op("act", lambda e, vi=vi, i=i: e.copy(out=vb[vi][:], in_=pp[i][:, 0:HL * 128]), reads=[("pp", i)],
                       writes=[("vb", vi)])
                    r0 = t * 512 + s * 128
                    dma("sp", vS[r0:r0 + 128, :], vb[vi][:], reads=[("vb", vi)], writes=["vS"])
                    hook()
                for c in range(2):
                    for j in range(31):
                        op("pe", lambda e, c=c, j=j: e.matmul(pcv[:], lhsT=diag[:, c * 31 + j, :],
                                                              rhs=cbuf[c][:, 2 + j:2 + j + 512],
                                                              start=(j == 0), stop=(j == 30)),
                           reads=["diag", ("cbuf", c)], writes=["pcv"], signal=(j == 30))
                    op("act", lambda e, c=c: e.activation(out=cv[sl][c][:], in_=pcv[:], func=AF.Identity,
                                                          bias=cvec_sb[:, c:c + 1]),
                       reads=["pcv", "cvec_sb"], writes=[("cv", sl, c)])
                    op("pool", lambda e, c=c: e.tensor_copy(out=cbuf[c][:, 0:32], in_=cbuf[c][:, 512:544]),
                       reads=[("cbuf", c)], writes=[("cbuf", c)])
                    hook()
                while any(k >= gcount[0] for k in hooks):
                    hook()

            def c2_pieces(t):
                sl = t % 2
                cols = slice(t * 512, (t + 1) * 512)

                def p0():
                    for c in range(2):
                        op("pe", lambda e, c=c: e.matmul(pst[:], lhsT=onesM[:], rhs=cv[sl][c][:], start=(c == 0),
                                                         stop=(c == 1)),
                           reads=["onesM", ("cv", sl, c)], writes=["pst"], signal=(c == 1))
                    for c in range(2):
                        op("dve", lambda e, c=c: e.tensor_tensor(out=cv[sl][c][:], in0=cv[sl][c][:], in1=pst[:],
                                                                 op=ALU.subtract),
                           reads=[("cv", sl, c), "pst"], writes=[("cv", sl, c)])
                        op("act", lambda e, c=c: e.activation(out=sq[c][:], in_=cv[sl][c][:], func=AF.Square),
                           reads=[("cv", sl, c)], writes=[("sq", c)])

                def p1():
                    for c in range(2):
                        op("pe", lambda e, c=c: e.matmul(pst[:], lhsT=onesM[:], rhs=sq[c][:], start=(c == 0),
                                                         stop=(c == 1)),
                           reads=["onesM", ("sq", c)], writes=["pst"], signal=(c == 1))
                    op("dve", lambda e: e.tensor_scalar(out=rstd[:], in0=pst[:], scalar1=LN_EPS, scalar2=None,
                                                        op0=ALU.add), reads=["pst"], writes=["rstd"])
                    op("act", lambda e: e.activation(out=rstd[:], in_=rstd[:], func=AF.Sqrt), reads=["rstd"],
                       writes=["rstd"])
                    op("dve", lambda e: e.reciprocal(out=rstd[:], in_=rstd[:]), reads=["rstd"], writes=["rstd"])

                def p2():
                    for c in range(2):
                        op("dve", lambda e, c=c: e.tensor_tensor(out=cv[sl][c][:], in0=cv[sl][c][:], in1=rstd[:],
                                                                 op=ALU.mult),
                           reads=[("cv", sl, c), "rstd"], writes=[("cv", sl, c)])
                        o = ob_i[0] % 4
                        ob_i[0] += 1
                        op("act", lambda e, c=c, o=o: e.activation(out=ob[o][:], in_=cv[sl][c][:], func=AF.Silu,
                                                                   scale=cvec_sb[:, 2 + c:3 + c],
                                                                   bias=cvec_sb[:, 4 + c:5 + c]),
                           reads=[("cv", sl, c), "cvec_sb"], writes=[("ob", o)])
                        dma("sp", yT[256 + c * 128:256 + (c + 1) * 128, cols], ob[o][:], reads=[("ob", o)],
                            writes=["yT"])
                return p0, p1, p2

            ya_toks = []
            cc_sem_a = nc.alloc_semaphore(f"cc_sema{l}")
            ab_load(0)
            ab_stats(0)
            for s in range(4):
                ab_sub(0, s)
            if NT > 1:
                ab_load(1)
            for t in range(NT):
                hooks = {}

                def add(gi, f):
                    hooks.setdefault(gi, []).append(f)
                if t >= 1:
                    p0, p1, p2 = c2_pieces(t - 1)
                    add(1, p0)
                    add(6, p1)
                    add(11, p2)
                if t + 1 < NT:
                    add(3, lambda t=t: ab_stats(t + 1))
                    for s in range(4):
                        add(11 + 4 * s, lambda t=t, s=s: ab_sub_a(t + 1, s))
                        add(15 + 4 * s, lambda t=t, s=s: ab_sub(t + 1, s, with_a=False))
                if t + 2 < NT:
                    add(13, lambda t=t: ab_load(t + 2, cs=False))
                c1(t, hooks)
                if t + 2 < NT:
                    ab_load(t + 2, x=False)
            for f in c2_pieces(NT - 1):
                f()
            sch.collective(ya_toks, lambda e: e.collective_compute(
                "AllGather", ALU.bypass, replica_groups=PAIRS,
                ins=[yasrc[l][:, :]], outs=[yadst[l][:, :]]), cc_sem_a)
            sch.end_phase()


        with ExitStack() as es:
            def sb(name, shape, dt=F32):
                return es.enter_context(nc.sbuf_tensor(f"{name}_L{l}", list(shape), dt))

            def ps(name, shape, dt=F32):
                return es.enter_context(nc.psum_tensor(f"{name}_L{l}", list(shape), dt))
            ksb = [sb(f"ksb{i}", [128, S], BF16) for i in range(2)]
            vsb = [sb(f"vsb{i}", [128, NKB, 128], BF16) for i in range(2)]
            qsb = [sb(f"qsb{i}", [128, 512], BF16) for i in range(3)]
            pb = [sb(f"pb{i}", [128, 2, 512], BF16) for i in range(3)]
            trif = sb("trif", [128, 128])
            tri2 = sb("tri2", [128, 2, 128], BF16)
            ones_b = sb("ones_b", [128, 128], BF16)
            onesE = sb("onesE", [128, 128])
            ones_f = sb("ones_f", [128, 128])
            acc2 = [sb(f"acc2_{i}", [128, 512]) for i in range(2)]
            epsc = sb("epsc", [128, 1])
            lam_sb = sb("lam_sb", [128, 256])
            lprod = sb("lprod", [128, 2, 64])
            lsum = sb("lsum", [128, 2])
            nlam = sb("nlam", [128, 1])
            sg_sb = sb("sg_sb", [128, 1])
            osb = [sb(f"osb{i}", [128, 2, 512]) for i in range(2)]
            lsb = [sb(f"lsb{i}", [128, 2, 512]) for i in range(2)]
            o1 = [sb(f"o1_{i}", [128, 512]) for i in range(2)]
            osq = [sb(f"osq{i}", [128, 512]) for i in range(2)]
            rstd = [sb(f"rstd2_{i}", [128, 512]) for i in range(2)]
            yb = [sb(f"yb{i}", [128, 512], BF16) for i in range(2)]
            ps_s = [ps(f"ps_s{i}", [128, 2, 512]) for i in range(2)]
            ps_o = ps("ps_o", [128, 2, 512])
            ps_l = ps("ps_l", [128, 2, 512])

            dma("sp", lam_sb[:], lamv[l], writes=["lam_sb"])
            dma("sp", sg_sb[:], subg[l], writes=["sg_sb"])
            op("pool", lambda e: e.memset(trif[:], 1.0), writes=["trif"])
            op("pool", lambda e: e.affine_select(out=trif[:], in_=trif[:], pattern=[[1, 128]],
                                                 compare_op=ALU.is_ge, fill=0.0, base=0,
                                                 channel_multiplier=-1), reads=["trif"], writes=["trif"])
            for mm in range(2):
                op("dve", lambda e, mm=mm: e.tensor_copy(out=tri2[:, mm, :], in_=trif[:]), reads=["trif"], writes=["tri2"])
            op("pool", lambda e: e.memset(ones_b[:], 1.0), writes=["ones_b"])
            op("pool", lambda e: e.memset(onesE[:], 1.0 / 128.0), writes=["onesE"])
            op("pool", lambda e: e.memset(ones_f[:], 1.0), writes=["ones_f"])
            op("pool", lambda e: e.memset(epsc[:], RMS_EPS), writes=["epsc"])
            lv = lam_sb[:].rearrange("p (a d) -> p a d", a=4)
            for i in range(2):
                op("dve", lambda e, i=i: e.tensor_tensor(out=lprod[:, i, :], in0=lv[:, 2 * i, :], in1=lv[:, 2 * i + 1, :],
                                                         op=ALU.mult), reads=["lam_sb"], writes=["lprod"])
            op("dve", lambda e: e.tensor_reduce(out=lsum[:], in_=lprod[:], axis=mybir.AxisListType.X, op=ALU.add),
               reads=["lprod"], writes=["lsum"])
            op("act", lambda e: e.activation(out=lsum[:], in_=lsum[:], func=AF.Exp), reads=["lsum"], writes=["lsum"])
            op("dve", lambda e: e.tensor_tensor(out=nlam[:], in0=lsum[:, 1:2], in1=lsum[:, 0:1], op=ALU.subtract),
               reads=["lsum"], writes=["nlam"])
            op("dve", lambda e: e.tensor_scalar(out=nlam[:], in0=nlam[:], scalar1=-lam_init, scalar2=None, op0=ALU.add),
               reads=["nlam"], writes=["nlam"])
            op("dve", lambda e: e.tensor_scalar(out=sg_sb[:], in0=sg_sb[:], scalar1=1.0 - lam_init, scalar2=None,
                                                op0=ALU.mult), reads=["sg_sb"], writes=["sg_sb"])

            def load_head(h):
                hs = h % 2
                dma("sp", ksb[hs][:], kT[h], writes=[("ksb", hs)])
                dma("sp", vsb[hs][:], vS[:, h * 128:(h + 1) * 128].rearrange("(kb p) e -> p kb e", p=128),
                    writes=[("vsb", hs)])

            def load_q(gidx):
                h, g = divmod(gidx, NT)
                qs = gidx % 3
                dma("sp", qsb[qs][:], qT[h, :, g * 512:(g + 1) * 512], writes=[("qsb", qs)])

            steps = []
            for gidx in range(HL * NT):
                g = gidx % NT
                for kb in range(4 * g + 4):
                    steps.append((gidx, kb))
            slot_of = {}
            cnt = {"si": 0, "pi": 0, "step": 0}
            deferred = []

            def emit_S(i):
                gidx, kb = steps[i]
                h, g = divmod(gidx, NT)
                hs, qs = h % 2, gidx % 3
                j = kb - 4 * g
                q0 = 128 * j if j > 0 else 0
                s_ = cnt["si"] % 2
                cnt["si"] += 1
                slot_of[i] = s_
                for mm in range(2):
                    lo = 64 * mm
                    op("pe", lambda e, mm=mm, lo=lo: e.matmul(
                        ps_s[s_][:, mm, q0:512], lhsT=ksb[hs][lo:lo + 64, kb * 128:(kb + 1) * 128],
                        rhs=qsb[qs][lo:lo + 64, q0:512], start=True, stop=True),
                       reads=[("ksb", hs), ("qsb", qs)], writes=[("ps_s", s_)], signal=(mm == 1))

            def emit_rest(i):
                gidx, kb = steps[i]
                h, g = divmod(gidx, NT)
                hs = h % 2
                nkb = 4 * g + 4
                j = kb - 4 * g
                q0 = 128 * j if j > 0 else 0
                s_ = slot_of.pop(i)
                p_ = cnt["pi"] % 3
                cnt["pi"] += 1
                op("act", lambda e: e.activation(out=pb[p_][:, :, q0:512], in_=ps_s[s_][:, :, q0:512],
                                                 func=AF.Exp, scale=0.125),
                   reads=[("ps_s", s_)], writes=[("pb", p_)])
                if j >= 0:
                    op("dve", lambda e: e.tensor_tensor(out=pb[p_][:, :, q0:q0 + 128], in0=pb[p_][:, :, q0:q0 + 128],
                                                        in1=tri2[:], op=ALU.mult),
                       reads=[("pb", p_), "tri2"], writes=[("pb", p_)])
                run_deferred()
                if i + 2 < len(steps):
                    emit_S(i + 2)
                first = (kb == 0)
                lastk = (kb == nkb - 1)
                for mm in range(2):
                    op("pe", lambda e, mm=mm: e.matmul(ps_o[:, mm, q0:512], lhsT=vsb[hs][:, kb, :],
                                                       rhs=pb[p_][:, mm, q0:512], start=first, stop=lastk),
                       reads=[("vsb", hs), ("pb", p_)], writes=["ps_o"], signal=False)
                op("pe", lambda e: e.matmul(ps_l[:, 0, q0:512], lhsT=ones_b[:], rhs=pb[p_][:, 0, q0:512],
                                            start=first, stop=lastk),
                   reads=["ones_b", ("pb", p_)], writes=["ps_l"], signal=True)
                pr = gidx % 2
                if first:
                    op("dve", lambda e: e.tensor_copy(out=acc2[pr][:], in_=pb[p_][:, 1, :]),
                       reads=[("pb", p_)], writes=[("acc2", pr)])
                else:
                    op("dve", lambda e: e.tensor_tensor(out=acc2[pr][:, q0:512], in0=acc2[pr][:, q0:512],
                                                        in1=pb[p_][:, 1, q0:512], op=ALU.add),
                       reads=[("pb", p_), ("acc2", pr)], writes=[("acc2", pr)])
                if lastk:
                    op("pe", lambda e: e.matmul(ps_l[:, 1, :], lhsT=ones_f[:], rhs=acc2[pr][:], start=True, stop=True),
                       reads=["ones_f", ("acc2", pr)], writes=["ps_l"], signal=True)
                    epilogue(gidx)

            def epilogue(gidx):
                h, g = divmod(gidx, NT)
                pr = gidx % 2
                op("act", lambda e: e.activation(out=lsb[pr][:], in_=ps_l[:], func=AF.Ln),
                   reads=["ps_l"], writes=[("lsb", pr)])
                op("act", lambda e: e.activation(out=lsb[pr][:], in_=lsb[pr][:], func=AF.Exp, scale=-1.0),
                   reads=[("lsb", pr)], writes=[("lsb", pr)])
                op("dve", lambda e: e.tensor_tensor(out=osb[pr][:], in0=ps_o[:], in1=lsb[pr][:], op=ALU.mult),
                   reads=["ps_o", ("lsb", pr)], writes=[("osb", pr)])
                op("dve", lambda e: e.scalar_tensor_tensor(out=o1[pr][:], in0=osb[pr][:, 1, :], scalar=nlam[:, 0:1],
                                                           in1=osb[pr][:, 0, :], op0=ALU.mult, op1=ALU.add),
                   reads=[("osb", pr), "nlam"], writes=[("o1", pr)])
                op("dve", lambda e: e.tensor_tensor(out=osq[pr][:], in0=o1[pr][:], in1=o1[pr][:], op=ALU.mult),
                   reads=[("o1", pr)], writes=[("osq", pr)])

                def part_c1():
                    s_ = cnt["si"] % 2
                    op("pe", lambda e: e.matmul(ps_s[s_][:, 0, :], lhsT=onesE[:], rhs=osq[pr][:], start=True, stop=True),
                       reads=["onesE", ("osq", pr)], writes=[("ps_s", s_)])

                    def part_c2():
                        op("act", lambda e: e.activation(out=rstd[pr][:], in_=ps_s[s_][:, 0, :], func=AF.Ln,
                                                         bias=epsc[:, 0:1]),
                           reads=[("ps_s", s_), "epsc"], writes=[("rstd", pr)])
                        op("act", lambda e: e.activation(out=rstd[pr][:], in_=rstd[pr][:], func=AF.Exp, scale=-0.5),
                           reads=[("rstd", pr)], writes=[("rstd", pr)])
                        op("dve", lambda e: e.scalar_tensor_tensor(out=yb[pr][:], in0=o1[pr][:], scalar=sg_sb[:, 0:1],
                                                                   in1=rstd[pr][:], op0=ALU.mult, op1=ALU.mult),
                           reads=[("o1", pr), "sg_sb", ("rstd", pr)], writes=[("yb", pr)])
                        tk = dma("sp", ybsrc[l][h][:, g * 512:(g + 1) * 512], yb[pr][:],
                                 reads=[("yb", pr)], writes=["ybsrc"])
                        yb_toks[h].append(tk)
                        if len(yb_toks[h]) == NT:
                            sch.collective(yb_toks[h], lambda e, h=h: e.collective_compute(
                                "AllGather", ALU.bypass, replica_groups=PAIRS,
                                ins=[ybsrc[l][h][:, :]], outs=[ybdst[l][h][:, :]]), cc_sem)
                    part_c2()
                deferred.append((cnt["step"] + 10, part_c1))

            def run_deferred(force=False):
                while True:
                    due = [d for d in deferred if force or d[0] <= cnt["step"]]
                    if not due:
                        break
                    d = due[0]
                    deferred.remove(d)
                    d[1]()

            yb_toks = [[] for _ in range(HL)]
            cc_sem = nc.alloc_semaphore(f"cc_sem{l}")
            load_head(0)
            load_q(0)
            load_q(1)
            emit_S(0)
            emit_S(1)
            for i, (gidx, kb) in enumerate(steps):
                cnt["step"] = i
                h, g = divmod(gidx, NT)
                if kb == 0:
                    if g == 0 and h + 1 < HL:
                        load_head(h + 1)
                    if gidx + 2 < HL * NT:
                        load_q(gidx + 2)
                emit_rest(i)
            cnt["step"] += 1000
            run_deferred(force=True)
            sch.pool_wait(cc_sem_a, 1)
            sch.pool_wait(cc_sem, HL)
            op("pool", lambda e: e.memset(epsc[:], RMS_EPS), writes=["epsc"])
            sch.end_phase()

        with ExitStack() as es:
            def sb(name, shape, dt=F32):
                return es.enter_context(nc.sbuf_tensor(f"{name}_L{l}", list(shape), dt))

            def ps(name, shape, dt=F32):
                return es.enter_context(nc.psum_tensor(f"{name}_L{l}", list(shape), dt))
            TT = 256
            NT3 = (S // 2) // TT
            wo = sb("wo", [128, 8, D], BF16)
            wg = sb("wg", [128, 8, FF], BF16)
            wu = sb("wu", [128, 8, FF], BF16)
            wd = sb("wd", [128, NFC, D], BF16)
            g_sb = sb("g3_sb", [128, 8])
            gf_sb = sb("gf_sb", [128, D]) if last else None
            identf = sb("identf3", [128, 128])
            ident = sb("ident3", [128, 128], BF16)
            mh = sb("mh3", [128, 4])
            xt = [sb(f"x3_{i}", [128, 2, D]) for i in range(2)]
            yt = [sb(f"y3_{i}", [128, 8, TT], BF16) for i in range(2)]
            ytc = sb("y3c", [128, 8, TT], BF16)
            rf_sb = sb("rf_sb", [128, 2])
            ss = sb("ss3", [128, 2])
            rs = sb("rs3", [128, 2])
            hb = sb("hb3", [128, D], BF16)
            hT = [sb(f"hT3_{i}", [128, 8, TT], BF16) for i in range(2)]
            aT = sb("aT", [128, NFC, TT], BF16)
            sg = [sb(f"sg{i}", [128, TT]) for i in range(2)]
            pmix = ps("pmix", [128, 2, 512])
            pT = ps("pT3", [128, D], BF16)
            pgu = [ps(f"pgu{i}", [128, 2, TT]) for i in range(2)]
            pdn = ps("pdn", [128, 2, 512])

            for k in range(8):
                dma("pool", wo[:, k, :], w_out[l, k * 128:(k + 1) * 128, :], writes=[("wo", k)])
            for k in range(8):
                dma("pool", wg[:, k, :], w_gate[l, k * 128:(k + 1) * 128, :], writes=[("wg", k)])
                dma("pool", wu[:, k, :], w_up[l, k * 128:(k + 1) * 128, :], writes=[("wu", k)])
            for fc in range(NFC):
                dma("pool", wd[:, fc, :], w_down[l, fc * 128:(fc + 1) * 128, :], writes=[("wd", fc)])
            dma("sp", g_sb[:], gcol[:, 2 * l + 1, :], writes=["g_sb"])
            dma("sp", rf_sb[:], rflag[:, :], writes=["rf_sb"])
            if last:
                dma("sp", gf_sb[:], gfin[:, :], writes=["gf_sb"])
            op("pool", lambda e: e.memset(identf[:], 1.0), writes=["identf"])
            op("pool", lambda e: e.affine_select(out=identf[:], in_=identf[:], pattern=[[-1, 128]],
                                                 compare_op=ALU.is_equal, fill=0.0, base=0,
                                                 channel_multiplier=1), reads=["identf"], writes=["identf"])
            op("dve", lambda e: e.tensor_copy(out=ident[:], in_=identf[:]), reads=["identf"], writes=["ident"])
            op("pool", lambda e: e.memset(mh[:], -0.5), writes=["mh"])

            yv = yT.rearrange("(k p) n -> p k n", p=128)
            yav = yadst[l].ap().rearrange("(k p) n -> p k n", p=128)
            def load_x(t):
                sl = t % 2
                dma("sp", xt[sl][:], x_tile256(t), writes=[("xt", sl)])

            def load_y(t):
                sl = t % 2
                for hf, (dstt, dn) in enumerate(((yt[sl], ("yt", sl)), (ytc, "ytc"))):
                    cs3 = slice(hf * (S // 2) + t * TT, hf * (S // 2) + (t + 1) * TT)
                    dma("sp", dstt[:, 0:2, :], yav[:, :, cs3], writes=[dn])
                    for gh in range(2 * HL):
                        rk, hh = divmod(gh, HL)
                        dma("sp", dstt[:, 2 + gh, :], ybdst[l][hh][rk * 128:(rk + 1) * 128, cs3], writes=[dn])
                    dma("sp", dstt[:, 6:8, :], yv[:, 2:4, cs3], writes=[dn])

            def blend_y(t):
                sl = t % 2
                op("dve", lambda e: e.tensor_scalar(out=yt[sl][:], in0=yt[sl][:], scalar1=rf_sb[:, 0:1], scalar2=None,
                                                    op0=ALU.mult), reads=[("yt", sl), "rf_sb"], writes=[("yt", sl)])
                op("dve", lambda e: e.scalar_tensor_tensor(out=yt[sl][:], in0=ytc[:], scalar=rf_sb[:, 1:2], in1=yt[sl][:],
                                                           op0=ALU.mult, op1=ALU.add),
                   reads=["ytc", ("yt", sl), "rf_sb"], writes=[("yt", sl)])

            gcnt = [0]

            def A1(t, s):
                sl = t % 2
                for half in range(2):
                    for k in range(8):
                        op("pe", lambda e, k=k, half=half: e.matmul(
                            pmix[:, half, :], lhsT=yt[sl][:, k, s * 128:(s + 1) * 128],
                            rhs=wo[:, k, half * 512:(half + 1) * 512], start=(k == 0), stop=(k == 7)),
                           reads=[("yt", sl), ("wo", k)], writes=["pmix"], signal=(k == 7 and half == 1))
                op("dve", lambda e: e.tensor_tensor(out=xt[sl][:, s, :], in0=xt[sl][:, s, :],
                                                    in1=pmix[:].rearrange("p a b -> p (a b)"), op=ALU.add),
                   reads=[("xt", sl), "pmix"], writes=[("xt", sl)])
                op("act", lambda e: e.activation(out=hb[:], in_=xt[sl][:, s, :], func=AF.Square,
                                                 accum_out=ss[:, s:s + 1]),
                   reads=[("xt", sl)], writes=["hb", ("ss", s)])
                op("dve", lambda e: e.tensor_scalar(out=rs[:, s:s + 1], in0=ss[:, s:s + 1], scalar1=1.0 / D,
                                                    scalar2=RMS_EPS, op0=ALU.mult, op1=ALU.add),
                   reads=[("ss", s)], writes=[("rs", s)])
                op("pool", lambda e: e.tensor_tensor(out=rs[:, s:s + 1], in0=rs[:, s:s + 1], in1=mh[:, 0:1],
                                                     op=ALU.pow), reads=[("rs", s), "mh"], writes=[("rs", s)])
                op("dve", lambda e: e.tensor_scalar(out=hb[:], in0=xt[sl][:, s, :], scalar1=rs[:, s:s + 1],
                                                    scalar2=None, op0=ALU.mult),
                   reads=[("xt", sl), ("rs", s)], writes=["hb"])

            def A2(t, s):
                sl = t % 2
                for k in range(8):
                    op("pe", lambda e, k=k: e.transpose(out=pT[:, k * 128:(k + 1) * 128],
                                                        in_=hb[:, k * 128:(k + 1) * 128], identity=ident[:]),
                       reads=["hb", "ident"], writes=["pT"], signal=(k == 7))
                for k in range(8):
                    op("dve", lambda e, k=k: e.tensor_scalar(
                        out=hT[sl][:, k, s * 128:(s + 1) * 128], in0=pT[:, k * 128:(k + 1) * 128],
                        scalar1=g_sb[:, k:k + 1], scalar2=None, op0=ALU.mult),
                       reads=["pT", "g_sb"], writes=[("hT", sl)])

            def p3_tile(t, hooks):
                sl = t % 2
                for fc in range(NFC):
                    gs = gcnt[0] % 2
                    gcnt[0] += 1
                    for (a, wt_, wn) in ((0, wg, "wg"), (1, wu, "wu")):
                        for k in range(8):
                            op("pe", lambda e, k=k, a=a, wt_=wt_, fc=fc, gs=gs: e.matmul(
                                pgu[gs][:, a, :], lhsT=wt_[:, k, fc * 128:(fc + 1) * 128], rhs=hT[sl][:, k, :],
                                start=(k == 0), stop=(k == 7)),
                               reads=[("hT", sl), (wn, k)], writes=[("pgu", gs)], signal=(k == 7 and a == 1))
                    op("act", lambda e, gs=gs: e.activation(out=sg[gs][:], in_=pgu[gs][:, 0, :], func=AF.Silu),
                       reads=[("pgu", gs)], writes=[("sg", gs)])
                    op("dve", lambda e, gs=gs, fc=fc: e.tensor_tensor(out=aT[:, fc, :], in0=pgu[gs][:, 1, :],
                                                                      in1=sg[gs][:], op=ALU.mult),
                       reads=[("pgu", gs), ("sg", gs)], writes=["aT"])
                    for f in hooks.get(fc, []):
                        f()
                for s in range(2):
                    for half in range(2):
                        for fc in range(NFC):
                            op("pe", lambda e, fc=fc, s=s, half=half: e.matmul(
                                pdn[:, half, :], lhsT=aT[:, fc, s * 128:(s + 1) * 128],
                                rhs=wd[:, fc, half * 512:(half + 1) * 512], start=(fc == 0), stop=(fc == NFC - 1)),
                               reads=["aT", ("wd", fc)], writes=["pdn"], signal=(fc == NFC - 1 and half == 1))
                    op("dve", lambda e, s=s: e.tensor_tensor(out=xt[sl][:, s, :], in0=xt[sl][:, s, :],
                                                             in1=pdn[:].rearrange("p a b -> p (a b)"), op=ALU.add),
                       reads=[("xt", sl), "pdn"], writes=[("xt", sl)])
                    if last:
                        op("act", lambda e, s=s: e.activation(out=hb[:], in_=xt[sl][:, s, :], func=AF.Square,
                                                              accum_out=ss[:, s:s + 1]),
                           reads=[("xt", sl)], writes=["hb", ("ss", s)])
                        op("dve", lambda e, s=s: e.tensor_scalar(out=rs[:, s:s + 1], in0=ss[:, s:s + 1], scalar1=1.0 / D,
                                                                 scalar2=RMS_EPS, op0=ALU.mult, op1=ALU.add),
                           reads=[("ss", s)], writes=[("rs", s)])
                        op("pool", lambda e, s=s: e.tensor_tensor(out=rs[:, s:s + 1], in0=rs[:, s:s + 1], in1=mh[:, 0:1],
                                                                  op=ALU.pow), reads=[("rs", s), "mh"], writes=[("rs", s)])
                        op("dve", lambda e, s=s: e.scalar_tensor_tensor(out=xt[sl][:, s, :], in0=xt[sl][:, s, :],
                                                                        scalar=rs[:, s:s + 1], in1=gf_sb[:],
                                                                        op0=ALU.mult, op1=ALU.mult),
                           reads=[("xt", sl), ("rs", s), "gf_sb"], writes=[("xt", sl)])
                tk = dma("sp", x_tile256(t, store=True), xt[sl][:], reads=[("xt", sl)], writes=["xo"])
                if not last:
                    x_toks.append(tk)
                    per = XCH // 256
                    if (t + 1) % per == 0:
                        c = t // per
                        sch.collective(x_toks[-per:], lambda e, c=c: e.collective_compute(
                            "AllGather", ALU.bypass, replica_groups=PAIRS,
                            ins=[xsrc[l][c][:, :]], outs=[xg[l][c][:, :]]), cc_semx)

            x_toks = []
            cc_semx = nc.alloc_semaphore(f"cc_semx{l}")
            load_x(0)
            load_y(0)
            blend_y(0)
            for s in range(2):
                A1(0, s)
                A2(0, s)
            if NT3 > 1:
                load_y(1)
            for t in range(NT3):
                hooks = {}
                if t + 1 < NT3:
                    load_x(t + 1)
                    hooks[3] = [lambda t=t: blend_y(t + 1)]
                    hooks[7] = [lambda t=t: A1(t + 1, 0)]
                    hooks[10] = [lambda t=t: A2(t + 1, 0)]
                    hooks[13] = [lambda t=t: A1(t + 1, 1)]
                    hooks[16] = [lambda t=t: A2(t + 1, 1)]
                if t + 2 < NT3:
                    hooks.setdefault(8, []).append(lambda t=t: load_y(t + 2))
                p3_tile(t, hooks)
            if not last:
                sch.pool_wait(cc_semx, NXC)
                op("pool", lambda e: e.memset(mh[:], -0.5), writes=["mh"])
            sch.end_phase()

    return nc


def _prep(inputs, S):
    f32 = np.float32
    depth = inputs["w_in"].shape[0]
    offs = np.cumsum([0, 256, 256, 256, 512, 512, 512, 512])
    a0, q0, k0, v0, c0 = 0, offs[3], offs[4], offs[5], offs[6]
    idx = list(range(c0, c0 + 512))

    def head_cols(base, heads):
        cols = []
        for h in heads:
            for mm in range(2):
                st = base + mm * 256 + h * 64
                cols += list(range(st, st + 64))
        return cols
    w_in_rank = []
    for r in range(2):
        heads = [HL * r + i for i in range(HL)]
        ida = [a0 + part * 256 + r * 128 + i for part in range(3) for i in range(128)]
        idr = ida + list(idx) + head_cols(q0, heads) + head_cols(k0, heads)
        for h in heads:
            idr += list(range(v0 + h * 128, v0 + (h + 1) * 128))
        idr = np.asarray(idr)
        assert idr.size == WIN_EXT
        w_in_rank.append(np.ascontiguousarray(np.asarray(inputs["w_in"], f32)[:, :, idr]))
    w_in_ext = w_in_rank[0]

    def cols128(v):
        v = np.asarray(v, f32)
        return np.ascontiguousarray(v.reshape(-1, 128).T)

    gcol = np.stack([cols128(inputs["mix_norm_g"][l]) if i == 0 else cols128(inputs["ffn_norm_g"][l])
                     for l in range(depth) for i in range(2)], axis=1)
    gfin = np.ascontiguousarray(np.broadcast_to(np.asarray(inputs["final_norm_g"], f32)[None, :], (128, D)))
    scw = np.stack([np.concatenate([np.stack([np.asarray(inputs["short_conv_w"], f32)[l, j, c * 128:(c + 1) * 128]
                                              for j in range(3)], axis=1) for c in range(2)], axis=1)
                    for l in range(depth)])
    glub = np.stack([cols128(inputs["glu_b"][l]) for l in range(depth)])
    cdw = np.stack([np.concatenate([np.stack([np.asarray(inputs["conf_dw_w"], f32)[l, j, c * 128:(c + 1) * 128]
                                              for j in range(31)], axis=1) for c in range(2)], axis=1)
                    for l in range(depth)])
    cvec = np.stack([np.concatenate([cols128(inputs["conf_dw_b"][l]), cols128(inputs["conf_ln_g"][l]),
                                     cols128(inputs["conf_ln_b"][l])], axis=1) for l in range(depth)])
    lamv = np.stack([np.broadcast_to(np.concatenate([np.asarray(inputs[k], f32)[l] for k in
                                                     ("lam_q1", "lam_k1", "lam_q2", "lam_k2")])[None, :], (128, 256))
                     for l in range(depth)])
    subg = np.stack([np.asarray(inputs["diff_subln_g"], f32)[l].reshape(128, 1) for l in range(depth)])
    inv_freq = (1.0 / (np.float32(10000.0) ** (np.arange(0, 64, 2, dtype=np.float32) / np.float32(64)))).astype(f32)
    p = np.arange(128)
    cst = np.zeros((128, 4), f32)
    cst[:, 0] = inv_freq[p % 32]
    cst[:, 1] = np.where((p % 64) < 32, -1.0, 1.0)
    gtok = np.ascontiguousarray(np.broadcast_to(np.asarray(inputs["mix_norm_g"], f32)[:, None, :], (depth, 128, D)))
    shared = dict(cst=cst, gcol=np.ascontiguousarray(gcol), gfin=gfin, gtok=gtok, w_in=w_in_ext,
                  w_out=np.ascontiguousarray(np.asarray(inputs["w_out"], f32)),
                  w_gate=np.ascontiguousarray(np.asarray(inputs["w_gate"], f32)),
                  w_up=np.ascontiguousarray(np.asarray(inputs["w_up"], f32)),
                  w_down=np.ascontiguousarray(np.asarray(inputs["w_down"], f32)),
                  scw=np.ascontiguousarray(scw), glub=np.ascontiguousarray(glub), cdw=np.ascontiguousarray(cdw),
                  cvec=np.ascontiguousarray(cvec), lamv=np.ascontiguousarray(lamv), subg=np.ascontiguousarray(subg))
    x = np.asarray(inputs["x"], f32)
    pos = np.asarray(inputs["positions"]).astype(np.int32)
    maps = []
    for b in range(x.shape[0]):
        xb = np.ascontiguousarray(x[b])
        pb_ = np.ascontiguousarray(np.broadcast_to(pos[b][None, :], (128, S)))
        for r in range(2):
            mp = dict(shared)
            mp["w_in"] = w_in_rank[r]
            mp["scw"] = np.ascontiguousarray(scw[:, :, 3 * r:3 * r + 3])
            mp["x"] = xb
            mp["xh"] = np.ascontiguousarray(xb[r * (S // 2):(r + 1) * (S // 2)])
            fl = np.zeros((128, 2), f32)
            fl[:, r] = 1.0
            mp["rflag"] = fl
            mp["pos"] = pb_
            maps.append(mp)
    return maps


_NC_CACHE = {}


def kernel(**inputs):
    x = np.asarray(inputs["x"])
    B, S, _ = x.shape
    depth = inputs["w_in"].shape[0]
    maps = _prep(inputs, S)
    n_cores = 8
    in_maps = [maps[c % (2 * B)] for c in range(n_cores)]
    key = (S, depth)
    if key not in _NC_CACHE:
        _NC_CACHE[key] = build_nc(S, depth)
    nc = _NC_CACHE[key]
    res = run_bass_kernel_spmd(nc, in_maps, core_ids=list(range(n_cores)))
    return np.stack([np.concatenate([np.asarray(res.results[2 * b + r]["out"], np.float32) for r in range(2)], axis=0)
                     for b in range(B)], axis=0)
```

```python
import math
from contextlib import ExitStack

import numpy as np
import concourse.bass as bass
import concourse.mybir as mybir
from concourse.bass_utils import run_bass_kernel_spmd

F32 = mybir.dt.float32
BF16 = mybir.dt.bfloat16
I32 = mybir.dt.int32
ALU = mybir.AluOpType
AF = mybir.ActivationFunctionType

D = 1024
DEPTH = 2
FF = 2816
NFC = FF // 128
HL = 2
WIN_EXT = 1664
G_AB, G_AC, G_AX, G_CV, G_CG, G_Q, G_K = 0, 1, 2, 3, 5, 7, 9
V_OFF = 11 * 128
PAIRS = [[0, 1], [2, 3], [4, 5], [6, 7]]
RMS_EPS = 1e-6
LN_EPS = 1e-5
TWO_PI = 2.0 * math.pi
C1 = 6.28125
C2 = TWO_PI - C1


class Sched:
    ENGS = ("pe", "act", "dve", "pool", "sp")

    def __init__(self, nc, n_dma_sems=32, same_engine_sync=True):
        self.nc = nc
        self.same_engine_sync = same_engine_sync
        self.n_dma_sems = n_dma_sems
        self.dma_sems = [nc.alloc_semaphore(f"dq{i}") for i in range(n_dma_sems)]
        self.dma_cnt = [0] * n_dma_sems
        self.dma_last = [None] * n_dma_sems
        self.dma_i = 0
        self.dma_i_sw = 0
        self.phase = -1
        self.res_w = {}
        self.res_r = {}
        self.n_ops = 0
        self.new_phase()

    def new_phase(self):
        self.phase += 1
        self.prog = {e: [] for e in self.ENGS}
        self.esem = {e: self.nc.alloc_semaphore(f"s_{e}_{self.phase}") for e in self.ENGS}
        self.seq = {e: 0 for e in self.ENGS}
        self.sig = {e: 0 for e in self.ENGS}
        self.seq2sig = {e: [] for e in self.ENGS}
        self.pending = {e: [] for e in self.ENGS}
        self.seen = {e: {} for e in self.ENGS}
        for e in self.ENGS:
            for j in range(self.n_dma_sems):
                self.seen[e][("d", j)] = 16 * self.dma_cnt[j]

    def _resolve(self, tok):
        if tok[0] == "d":
            return (("d", tok[1]), self.dma_sems[tok[1]], tok[2])
        _, eng, ph, seq = tok
        v = self.seq2sig[eng][seq]
        assert v is not None, f"dependency on unsignaled instruction {tok}"
        return (("e", eng), self.esem[eng], v)

    def _deps(self, eng, reads, writes):
        toks = []
        for r in reads:
            t = self.res_w.get(r)
            if t is not None:
                toks.append(t)
        for w in writes:
            t = self.res_w.get(w)
            if t is not None:
                toks.append(t)
            toks.extend(self.res_r.get(w, []))
        waits = {}
        for t in toks:
            if t[0] == "e":
                if t[2] != self.phase:
                    continue
                if t[1] == eng and (not self.same_engine_sync or self.seq2sig[eng][t[3]] is None):
                    continue
            key, sem, v = self._resolve(t)
            if self.seen[eng].get(key, 0) >= v:
                continue
            if key not in waits or waits[key][1] < v:
                waits[key] = (sem, v)
        for key, (sem, v) in waits.items():
            self.seen[eng][key] = v
        return list(waits.values())

    def _record(self, tok, reads, writes):
        for r in reads:
            self.res_r.setdefault(r, []).append(tok)
        for w in writes:
            self.res_w[w] = tok
            self.res_r[w] = []

    def op(self, eng, fn, reads=(), writes=(), signal=True):
        waits = self._deps(eng, reads, writes)
        seq = self.seq[eng]
        self.seq[eng] += 1
        tok = ("e", eng, self.phase, seq)
        self.seq2sig[eng].append(None)
        self.pending[eng].append(seq)
        if signal:
            self.sig[eng] += 1
            for s in self.pending[eng]:
                self.seq2sig[eng][s] = self.sig[eng]
            self.pending[eng] = []
        sem = self.esem[eng]

        def thunk(e, fn=fn, waits=waits, signal=signal, sem=sem):
            for (s, v) in waits:
                e.wait_ge(s, v)
            ins = fn(e)
            if signal:
                ins.then_inc(sem, 1)
        self.prog[eng].append(thunk)
        self._record(tok, reads, writes)
        self.n_ops += 1
        return tok

    def dma(self, eng, out, in_, reads=(), writes=(), **kw):
        n_sw = 8
        if eng == "pool":
            j = self.dma_i_sw % n_sw
            self.dma_i_sw += 1
        else:
            j = n_sw + self.dma_i % (self.n_dma_sems - n_sw)
            self.dma_i += 1
        waits = self._deps(eng, reads, writes)
        prev = self.dma_last[j]
        if prev is not None:
            key, sem, v = self._resolve(prev)
            if self.seen[eng].get(key, 0) < v:
                waits.append((sem, v))
                self.seen[eng][key] = v
        self.dma_cnt[j] += 1
        val = 16 * self.dma_cnt[j]
        tok = ("d", j, val)
        self.dma_last[j] = tok
        sem = self.dma_sems[j]

        def thunk(e, waits=waits, sem=sem, out=out, in_=in_, kw=kw):
            for (s, v) in waits:
                e.wait_ge(s, v)
            e.dma_start(out=out, in_=in_, **kw).then_inc(sem, 16)
        self.prog[eng].append(thunk)
        self._record(tok, reads, writes)
        self.n_ops += 1
        return tok

    def collective(self, wait_toks, emit_fn, cc_sem):
        waits = [self._resolve(t)[1:] for t in wait_toks]

        def thunk(e, waits=waits, emit_fn=emit_fn, cc_sem=cc_sem):
            for (s_, v) in waits:
                e.wait_ge(s_, v)
            emit_fn(e).then_inc(cc_sem)
        self.prog["pool"].append(thunk)

    def pool_wait(self, sem, val):
        def thunk(e, sem=sem, val=val):
            e.wait_ge(sem, val)
        self.prog["pool"].append(thunk)

    def end_phase(self):
        for e in self.ENGS:
            assert not self.pending[e], f"unsignaled trailing instructions on {e}"
        targets = []
        for e in self.ENGS:
            if self.sig[e] > 0:
                targets.append((self.esem[e], self.sig[e]))
        for j in range(self.n_dma_sems):
            if self.dma_cnt[j] > 0:
                targets.append((self.dma_sems[j], 16 * self.dma_cnt[j]))
        prog = self.prog
        nc = self.nc

        def run(e, name):
            for t in prog[name]:
                t(e)
            for (s, v) in targets:
                e.wait_ge(s, v)

        with nc.Block() as block:
            @block.tensor
            def _(e):
                run(e, "pe")

            @block.scalar
            def _(e):
                run(e, "act")

            @block.vector
            def _(e):
                run(e, "dve")

            @block.gpsimd
            def _(e):
                run(e, "pool")

            @block.sync
            def _(e):
                run(e, "sp")
        self.res_w = {}
        self.res_r = {}
        self.dma_last = [None] * self.n_dma_sems
        self.new_phase()


def build_nc(S, depth=DEPTH):
    NT = S // 512
    NKB = S // 128
    nc = bass.Bass("TRN2", target_bir_lowering=False)

    def din(name, shape, dt=F32):
        return nc.dram_tensor(name, list(shape), dt, kind="ExternalInput").ap()

    def dscr(name, shape, dt):
        return nc.dram_tensor(name, list(shape), dt, kind="Internal").ap()

    x_in = din("x", [S, D])
    xh_in = din("xh", [S // 2, D])
    rflag = din("rflag", [128, 2])
    pos_in = din("pos", [128, S], I32)
    cst = din("cst", [128, 4])
    gcol = din("gcol", [128, 2 * depth, 8])
    gfin = din("gfin", [128, D])
    gtok = din("gtok", [depth, 128, D])
    w_in = din("w_in", [depth, D, WIN_EXT])
    w_out = din("w_out", [depth, D, D])
    w_gate = din("w_gate", [depth, D, FF])
    w_up = din("w_up", [depth, D, FF])
    w_down = din("w_down", [depth, FF, D])
    scw = din("scw", [depth, 128, 3])
    glub = din("glub", [depth, 128, 4])
    cdw = din("cdw", [depth, 128, 2 * 31])
    cvec = din("cvec", [depth, 128, 6])
    lamv = din("lamv", [depth, 128, 4 * 64])
    subg = din("subg", [depth, 128, 1])
    out = nc.dram_tensor("out", [S // 2, D], F32, kind="ExternalOutput").ap()
    XCH = 512
    yasrc = [nc.dram_tensor(f"yasrc{i}", [128, S], BF16) for i in range(depth)]
    yadst = [nc.dram_tensor(f"yadst{i}", [256, S], BF16) for i in range(depth)]
    NXC = (S // 2) // XCH
    xsrc = [[nc.dram_tensor(f"xsrc{i}_{c}", [XCH, D], F32) for c in range(NXC)] for i in range(depth - 1)]
    xg = [[nc.dram_tensor(f"xg{i}_{c}", [2 * XCH, D], F32) for c in range(NXC)] for i in range(depth - 1)]

    qT = dscr("qT", [HL, 128, S], BF16)
    kT = dscr("kT", [HL, 128, S], BF16)
    vS = dscr("vS", [S, HL * 128], BF16)
    ybsrc = [[nc.dram_tensor(f"ybsrc{i}_{h}", [128, S], BF16) for h in range(HL)] for i in range(depth)]
    ybdst = [[nc.dram_tensor(f"ybdst{i}_{h}", [2 * 128, S], BF16) for h in range(HL)] for i in range(depth)]
    yT = dscr("yT", [512, S], BF16)
    cosT = dscr("cosT", [128, S], F32)
    sinT = dscr("sinT", [128, S], F32)

    sch = Sched(nc)
    op, dma = sch.op, sch.dma

    CH = min(S, 2048)
    es_w0 = ExitStack()
    wsb_pre = es_w0.enter_context(nc.sbuf_tensor("wsb_pre", [128, 8, WIN_EXT], BF16))
    with ExitStack() as es:
        def sb(name, shape, dt=F32):
            return es.enter_context(nc.sbuf_tensor(name, list(shape), dt))
        for k in range(8):
            dma("pool", wsb_pre[:, k, :], w_in[0, k * 128:(k + 1) * 128, :], writes=[("wsb", k)])
        c_sb = sb("c_sb", [128, 4])
        posi = sb("posi", [128, CH], I32)
        ang = sb("ang", [128, CH])
        kf = sb("kf", [128, CH])
        ki = sb("ki", [128, CH], I32)
        r = sb("r", [128, CH])
        m = sb("m", [128, CH])
        rc = sb("rc", [128, CH])
        so = sb("so", [128, CH])
        co = sb("co", [128, CH])
        dma("sp", c_sb[:], cst[:, :], writes=["c_sb"])
        for c in range(S // CH):
            cs = slice(c * CH, (c + 1) * CH)
            dma("sp", posi[:], pos_in[:, cs], writes=["posi"])
            op("dve", lambda e: e.tensor_copy(out=ang[:], in_=posi[:]), reads=["posi"], writes=["ang"])
            op("dve", lambda e: e.tensor_scalar(out=ang[:], in0=ang[:], scalar1=c_sb[:, 0:1], scalar2=None,
                                                op0=ALU.mult), reads=["ang", "c_sb"], writes=["ang"])
            op("dve", lambda e: e.tensor_scalar(out=kf[:], in0=ang[:], scalar1=1.0 / TWO_PI, scalar2=None,
                                                op0=ALU.mult), reads=["ang"], writes=["kf"])
            op("dve", lambda e: e.tensor_copy(out=ki[:], in_=kf[:]), reads=["kf"], writes=["ki"])
            op("dve", lambda e: e.tensor_copy(out=kf[:], in_=ki[:]), reads=["ki"], writes=["kf"])
            op("dve", lambda e: e.scalar_tensor_tensor(out=r[:], in0=kf[:], scalar=-C1, in1=ang[:],
                                                       op0=ALU.mult, op1=ALU.add), reads=["kf", "ang"], writes=["r"])
            op("dve", lambda e: e.scalar_tensor_tensor(out=r[:], in0=kf[:], scalar=-C2, in1=r[:],
                                                       op0=ALU.mult, op1=ALU.add), reads=["kf", "r"], writes=["r"])

            def fold(dst, dn):
                op("dve", lambda e: e.tensor_scalar(out=m[:], in0=dst[:], scalar1=math.pi, scalar2=TWO_PI,
                                                    op0=ALU.is_gt, op1=ALU.mult), reads=[dn], writes=["m"])
                op("dve", lambda e: e.tensor_tensor(out=dst[:], in0=dst[:], in1=m[:], op=ALU.subtract),
                   reads=[dn, "m"], writes=[dn])
                op("dve", lambda e: e.tensor_scalar(out=dst[:], in0=dst[:], scalar1=-math.pi, scalar2=math.pi,
                                                    op0=ALU.max, op1=ALU.min), reads=[dn], writes=[dn])
            fold(r, "r")
            op("dve", lambda e: e.tensor_scalar(out=rc[:], in0=r[:], scalar1=math.pi / 2, scalar2=None,
                                                op0=ALU.add), reads=["r"], writes=["rc"])
            fold(rc, "rc")
            op("dve", lambda e: e.tensor_scalar(out=r[:], in0=r[:], scalar1=c_sb[:, 1:2], scalar2=None,
                                                op0=ALU.mult), reads=["r", "c_sb"], writes=["r"])
            op("act", lambda e: e.activation(out=so[:], in_=r[:], func=AF.Sin), reads=["r"], writes=["so"])
            op("act", lambda e: e.activation(out=co[:], in_=rc[:], func=AF.Sin), reads=["rc"], writes=["co"])
            dma("sp", sinT[:, cs], so[:], reads=["so"], writes=["sinT"])
            dma("sp", cosT[:, cs], co[:], reads=["co"], writes=["cosT"])
        sch.end_phase()

    for l in range(depth):
        def x_tile512(t, l=l):
            if l == 0:
                return x_in.rearrange("(t s p) d -> t p s d", p=128, s=4)[t]
            part, c = divmod(t, NXC)
            return xg[l - 1][c][part * XCH:(part + 1) * XCH, :].rearrange("(s p) d -> p s d", p=128)

        def x_tile256(t, l=l, store=False):
            if store:
                if l == depth - 1:
                    return out.rearrange("(t s p) d -> t p s d", p=128, s=2)[t]
                c, o = divmod(t, XCH // 256)
                return xsrc[l][c][o * 256:(o + 1) * 256, :].rearrange("(s p) d -> p s d", p=128)
            if l == 0:
                return xh_in.rearrange("(t s p) d -> t p s d", p=128, s=2)[t]
            c, o = divmod(t, XCH // 256)
            return xsrc[l - 1][c][o * 256:(o + 1) * 256, :].rearrange("(s p) d -> p s d", p=128)
        lam_init = 0.8 - 0.6 * math.exp(-0.3 * l)
        last = (l == depth - 1)

        with ExitStack() as es:
            def sb(name, shape, dt=F32):
                return es.enter_context(nc.sbuf_tensor(f"{name}_L{l}", list(shape), dt))

            def ps(name, shape, dt=F32):
                return es.enter_context(nc.psum_tensor(f"{name}_L{l}", list(shape), dt))
            wsb = wsb_pre if l == 0 else sb("wsb", [128, 8, WIN_EXT], BF16)
            g_sb = sb("g_sb", [128, 8])
            scw_sb = sb("scw_sb", [128, 3])
            glub_sb = sb("glub_sb", [128, 4])
            cdw_sb = sb("cdw_sb", [128, 62])
            cvec_sb = sb("cvec_sb", [128, 6])
            identf = sb("identf", [128, 128])
            ident = sb("ident", [128, 128], BF16)
            diag = sb("diag", [128, 62, 128], BF16)
            onesM = sb("onesM", [128, 128])
            pswap = sb("pswap", [128, 128], BF16)
            qb = [sb(f"qb{i}", [128, 512], BF16) for i in range(2)]
            mh = sb("mh", [128, 512])
            xt = [sb(f"xt{i}", [128, 4, D]) for i in range(2)]
            ss = [sb(f"ss{i}", [128, 4]) for i in range(2)]
            rs = [sb(f"rs{i}", [128, 4]) for i in range(2)]
            hb = [sb(f"hb{i}", [128, D], BF16) for i in range(2)]
            gt_sb = sb("gt_sb", [128, D])
            junk = sb("junk", [128, D], BF16)
            hT = [sb(f"hT{i}", [128, 8, 512], BF16) for i in range(2)]
            cs_sb = [sb(f"cos{i}", [128, 512]) for i in range(2)]
            sn_sb = [sb(f"sin{i}", [128, 512]) for i in range(2)]
            zc = [sb(f"zc{c}", [128, 2 + 512]) for c in range(2)]
            cbuf = [sb(f"cbuf{c}", [128, 32 + 512], BF16) for c in range(2)]
            tA = sb("tA", [128, 512])
            tB = sb("tB", [128, 512])
            tS = sb("tS", [128, 512])
            t1 = [sb(f"t1_{i}", [128, 512]) for i in range(2)]
            t2 = [sb(f"t2_{i}", [128, 512]) for i in range(2)]
            cv = [[sb(f"cv{i}_{c}", [128, 512]) for c in range(2)] for i in range(2)]
            sq = [sb(f"sq{c}", [128, 512]) for c in range(2)]
            rstd = sb("rstd", [128, 512])
            ob = [sb(f"ob{i}", [128, 512], BF16) for i in range(4)]
            vb = [sb(f"vb{i}", [128, HL * 128], BF16) for i in range(2)]
            pT = [ps(f"pT{i}", [128, D], BF16) for i in range(2)]
            pp = [ps(f"pp{i}", [128, 512]) for i in range(4)]
            pcv = ps("pcv", [128, 512])
            pst = ps("pst", [128, 512])

            for k in range(8 if l > 0 else 0):
                dma("pool", wsb[:, k, :], w_in[l, k * 128:(k + 1) * 128, :], writes=[("wsb", k)])
            dma("sp", gt_sb[:], gtok[l], writes=["gt_sb"])
            dma("sp", scw_sb[:], scw[l], writes=["scw_sb"])
            dma("sp", glub_sb[:], glub[l], writes=["glub_sb"])
            dma("sp", cdw_sb[:], cdw[l], writes=["cdw_sb"])
            dma("sp", cvec_sb[:], cvec[l], writes=["cvec_sb"])
            op("pool", lambda e: e.memset(identf[:], 1.0), writes=["identf"])
            op("pool", lambda e: e.affine_select(out=identf[:], in_=identf[:], pattern=[[-1, 128]],
                                                 compare_op=ALU.is_equal, fill=0.0, base=0,
                                                 channel_multiplier=1), reads=["identf"], writes=["identf"])
            op("dve", lambda e: e.tensor_copy(out=ident[:], in_=identf[:]), reads=["identf"], writes=["ident"])
            op("pool", lambda e: e.memset(onesM[:], 1.0 / 256.0), writes=["onesM"])
            op("pool", lambda e: e.memset(mh[:], -0.5), writes=["mh"])
            for b64 in range(2):
                for hh in range(2):
                    d0 = 64 * b64 + 32 * hh
                    s0 = 64 * b64 + 32 * (1 - hh)
                    op("dve", lambda e, d0=d0, s0=s0: e.tensor_copy(out=pswap[:, d0:d0 + 32], in_=identf[:, s0:s0 + 32]),
                       reads=["identf"], writes=["pswap"])
            for c in range(2):
                op("pool", lambda e, c=c: e.memset(zc[c][:, 0:2], 0.0), writes=[("zc", c)])
                op("pool", lambda e, c=c: e.memset(cbuf[c][:, 0:32], 0.0), writes=[("cbuf", c)])
            for i in range(62):
                op("dve", lambda e, i=i: e.tensor_scalar(out=diag[:, i, :], in0=identf[:], scalar1=cdw_sb[:, i:i + 1],
                                                         scalar2=None, op0=ALU.mult),
                   reads=["identf", "cdw_sb"], writes=["diag"])

            pp_i = [0]
            ob_i = [0]

            def ab_load(t, x=True, cs=True):
                sl = t % 2
                if x:
                    dma("sp", xt[sl][:], x_tile512(t), writes=[("xt", sl)])
                if cs:
                    dma("sp", cs_sb[sl][:], cosT[:, t * 512:(t + 1) * 512], writes=[("cos", sl)])
                    dma("sp", sn_sb[sl][:], sinT[:, t * 512:(t + 1) * 512], writes=[("sin", sl)])

            def ab_stats(t):
                sl = t % 2
                for s in range(4):
                    op("act", lambda e, s=s: e.activation(out=junk[:], in_=xt[sl][:, s, :], func=AF.Square,
                                                          accum_out=ss[sl][:, s:s + 1]),
                       reads=[("xt", sl)], writes=["junk", ("ss", sl)])
                op("dve", lambda e: e.tensor_scalar(out=rs[sl][:], in0=ss[sl][:], scalar1=1.0 / D, scalar2=RMS_EPS,
                                                    op0=ALU.mult, op1=ALU.add),
                   reads=[("ss", sl)], writes=[("rs", sl)])
                op("pool", lambda e: e.tensor_tensor(out=rs[sl][:], in0=rs[sl][:], in1=mh[:, 0:4], op=ALU.pow),
                   reads=[("rs", sl), "mh"], writes=[("rs", sl)])

            def ab_sub_a(t, s):
                sl = t % 2
                b_ = s % 2
                op("dve", lambda e: e.scalar_tensor_tensor(out=hb[b_][:], in0=xt[sl][:, s, :], scalar=rs[sl][:, s:s + 1],
                                                           in1=gt_sb[:], op0=ALU.mult, op1=ALU.mult),
                   reads=[("xt", sl), ("rs", sl), "gt_sb"], writes=[("hb", b_)])

            def ab_sub(t, s, with_a=True):
                sl = t % 2
                b_ = s % 2
                if with_a:
                    ab_sub_a(t, s)
                for k in range(8):
                    op("pe", lambda e, k=k: e.transpose(out=pT[b_][:, k * 128:(k + 1) * 128],
                                                        in_=hb[b_][:, k * 128:(k + 1) * 128], identity=ident[:]),
                       reads=[("hb", b_), "ident"], writes=[("pT", b_)], signal=(k == 7))
                op("act", lambda e: e.copy(out=hT[sl][:, :, s * 128:(s + 1) * 128],
                                           in_=pT[b_][:].rearrange("p (k c) -> p k c", k=8)),
                   reads=[("pT", b_)], writes=[("hT", sl)])

            def proj(t, gi):
                sl = t % 2
                i = pp_i[0] % 4
                pp_i[0] += 1
                for k in range(8):
                    op("pe", lambda e, k=k, i=i: e.matmul(pp[i][:], lhsT=wsb[:, k, gi * 128:(gi + 1) * 128],
                                                          rhs=hT[sl][:, k, :], start=(k == 0), stop=(k == 7)),
                       reads=[("hT", sl), ("wsb", k)], writes=[("pp", i)], signal=(k == 7))
                return pp[i], ("pp", i)

            def c1(t, hooks):
                sl = t % 2
                cols = slice(t * 512, (t + 1) * 512)
                gcount = [0]

                def hook():
                    for f in hooks.get(gcount[0], []):
                        f()
                    gcount[0] += 1
                for c in range(2):
                    pg, rg = proj(t, G_CG + c)
                    op("act", lambda e, pg=pg, c=c: e.activation(out=tS[:], in_=pg[:], func=AF.Sigmoid,
                                                                 bias=glub_sb[:, 2 + c:3 + c]),
                       reads=[rg, "glub_sb"], writes=["tS"])
                    hook()
                    pvl, rv = proj(t, G_CV + c)
                    op("dve", lambda e, pvl=pvl, c=c: e.scalar_tensor_tensor(
                        out=cbuf[c][:, 32:544], in0=pvl[:], scalar=glub_sb[:, c:c + 1], in1=tS[:],
                        op0=ALU.add, op1=ALU.mult), reads=[rv, "tS", "glub_sb"], writes=[("cbuf", c)])
                    hook()
                for c in range(1):
                    pc, rcn = proj(t, G_AC + c)
                    op("act", lambda e, pc=pc: e.copy(out=tA[:], in_=pc[:]), reads=[rcn], writes=["tA"])
                    hook()
                    px, rx = proj(t, G_AX + c)
                    op("dve", lambda e, px=px, c=c: e.tensor_tensor(out=zc[c][:, 2:514], in0=px[:], in1=tA[:],
                                                                    op=ALU.mult),
                       reads=[rx, "tA"], writes=[("zc", c)])
                    op("dve", lambda e, c=c: e.tensor_scalar(out=tB[:], in0=zc[c][:, 0:512],
                                                             scalar1=scw_sb[:, 3 * c:3 * c + 1], scalar2=None,
                                                             op0=ALU.mult),
                       reads=[("zc", c), "scw_sb"], writes=["tB"])
                    for j in (1, 2):
                        op("dve", lambda e, c=c, j=j: e.scalar_tensor_tensor(
                            out=tB[:], in0=zc[c][:, j:j + 512], scalar=scw_sb[:, 3 * c + j:3 * c + j + 1],
                            in1=tB[:], op0=ALU.mult, op1=ALU.add),
                           reads=[("zc", c), "scw_sb", "tB"], writes=["tB"])
                    op("dve", lambda e, c=c: e.tensor_copy(out=zc[c][:, 0:2], in_=zc[c][:, 512:514]),
                       reads=[("zc", c)], writes=[("zc", c)])
                    hook()
                    pb_, rb = proj(t, G_AB + c)
                    o = ob_i[0] % 4
                    ob_i[0] += 1
                    op("dve", lambda e, pb_=pb_, o=o: e.tensor_tensor(out=ob[o][:], in0=pb_[:], in1=tB[:], op=ALU.mult),
                       reads=[rb, "tB"], writes=[("ob", o)])
                    tk = dma("sp", yasrc[l][:, cols], ob[o][:], reads=[("ob", o)], writes=["yasrc"])
                    ya_toks.append(tk)
                    for _ in range(4):
                        hook()
                pend = [None]

                def rope_tail(pq, rq, h, i2, dst, qi):
                    def f():
                        i = pp_i[0] % 4
                        pp_i[0] += 1
                        op("pe", lambda e, i=i: e.matmul(pp[i][:], lhsT=pswap[:], rhs=qb[qi][:], start=True, stop=True),
                           reads=["pswap", ("qb", qi)], writes=[("pp", i)])
                        op("dve", lambda e, i=i: e.tensor_tensor(out=t2[i2][:], in0=pp[i][:], in1=sn_sb[sl][:], op=ALU.mult),
                           reads=[("pp", i), ("sin", sl)], writes=[("t2", i2)])
                        o = ob_i[0] % 4
                        ob_i[0] += 1
                        op("pool", lambda e, o=o: e.tensor_tensor(out=ob[o][:], in0=t1[i2][:], in1=t2[i2][:], op=ALU.add),
                           reads=[("t1", i2), ("t2", i2)], writes=[("ob", o)])
                        dma("sp", dst[h, :, cols], ob[o][:], reads=[("ob", o)], writes=[dst.name])
                    return f
                qcnt = 0
                for (g0, dst) in ((G_Q, qT), (G_K, kT)):
                    for h in range(HL):
                        i2 = qcnt % 2
                        qi = qcnt % 2
                        qcnt += 1
                        pq, rq = proj(t, g0 + h)
                        op("dve", lambda e, pq=pq, qi=qi: e.tensor_copy(out=qb[qi][:], in_=pq[:]), reads=[rq],
                           writes=[("qb", qi)])
                        op("dve", lambda e, pq=pq, i2=i2: e.tensor_tensor(out=t1[i2][:], in0=pq[:], in1=cs_sb[sl][:],
                                                                          op=ALU.mult),
                           reads=[rq, ("cos", sl), ("qb", qi)], writes=[("t1", i2)])
                        if pend[0] is not None:
                            pend[0]()
                        pend[0] = rope_tail(pq, rq, h, i2, dst, qi)
                        for _ in range(4):
                            hook()
                pend[0]()
                for s in range(4):
                    i = pp_i[0] % 4
                    pp_i[0] += 1
                    for k in range(8):
                        op("pe", lambda e, k=k, s=s, i=i: e.matmul(pp[i][:, 0:HL * 128], lhsT=hT[sl][:, k, s * 128:(s + 1) * 128],
                                                                   rhs=wsb[:, k, V_OFF:V_OFF + HL * 128],
                                                                   start=(k == 0), stop=(k == 7)),
                           reads=[("hT", sl), ("wsb", k)], writes=[("pp", i)], signal=(k == 7))
                    vi = s % 2
                    op("act", lambda e, vi=vi, i=i: e.copy(out=vb[vi][:], in_=pp[i][:, 0:HL * 128]), reads=[("pp", i)],
                       writes=[("vb", vi)])
                    r0 = t * 512 + s * 128
                    dma("sp", vS[r0:r0 + 128, :], vb[vi][:], reads=[("vb", vi)], writes=["vS"])
                    hook()
                for c in range(2):
                    for j in range(31):
                        op("pe", lambda e, c=c, j=j: e.matmul(pcv[:], lhsT=diag[:, c * 31 + j, :],
                                                              rhs=cbuf[c][:, 2 + j:2 + j + 512],
                                                              start=(j == 0), stop=(j == 30)),
                           reads=["diag", ("cbuf", c)], writes=["pcv"], signal=(j == 30))
                    op("act", lambda e, c=c: e.activation(out=cv[sl][c][:], in_=pcv[:], func=AF.Identity,
                                                          bias=cvec_sb[:, c:c + 1]),
                       reads=["pcv", "cvec_sb"], writes=[("cv", sl, c)])
                    op("pool", lambda e, c=c: e.tensor_copy(out=cbuf[c][:, 0:32], in_=cbuf[c][:, 512:544]),
                       reads=[("cbuf", c)], writes=[("cbuf", c)])
                    hook()
                while any(k >= gcount[0] for k in hooks):
                    hook()

            def c2_pieces(t):
                sl = t % 2
                cols = slice(t * 512, (t + 1) * 512)

                def p0():
                    for c in range(2):
                        op("pe", lambda e, c=c: e.matmul(pst[:], lhsT=onesM[:], rhs=cv[sl][c][:], start=(c == 0),
                                                         stop=(c == 1)),
                           reads=["onesM", ("cv", sl, c)], writes=["pst"], signal=(c == 1))
                    for c in range(2):
                        op("dve", lambda e, c=c: e.tensor_tensor(out=cv[sl][c][:], in0=cv[sl][c][:], in1=pst[:],
                                                                 op=ALU.subtract),
                           reads=[("cv", sl, c), "pst"], writes=[("cv", sl, c)])
                        op("act", lambda e, c=c: e.activation(out=sq[c][:], in_=cv[sl][c][:], func=AF.Square),
                           reads=[("cv", sl, c)], writes=[("sq", c)])

                def p1():
                    for c in range(2):
                        op("pe", lambda e, c=c: e.matmul(pst[:], lhsT=onesM[:], rhs=sq[c][:], start=(c == 0),
                                                         stop=(c == 1)),
                           reads=["onesM", ("sq", c)], writes=["pst"], signal=(c == 1))
                    op("dve", lambda e: e.tensor_scalar(out=rstd[:], in0=pst[:], scalar1=LN_EPS, scalar2=None,
                                                        op0=ALU.add), reads=["pst"], writes=["rstd"])
                    op("act", lambda e: e.activation(out=rstd[:], in_=rstd[:], func=AF.Sqrt), reads=["rstd"],
                       writes=["rstd"])
                    op("dve", lambda e: e.reciprocal(out=rstd[:], in_=rstd[:]), reads=["rstd"], writes=["rstd"])

                def p2():
                    for c in range(2):
                        op("dve", lambda e, c=c: e.tensor_tensor(out=cv[sl][c][:], in0=cv[sl][c][:], in1=rstd[:],
                                                                 op=ALU.mult),
                           reads=[("cv", sl, c), "rstd"], writes=[("cv", sl, c)])
                        o = ob_i[0] % 4
                        ob_i[0] += 1
                        op("act", lambda e, c=c, o=o: e.activation(out=ob[o][:], in_=cv[sl][c][:], func=AF.Silu,
                                                                   scale=cvec_sb[:, 2 + c:3 + c],
                                                                   bias=cvec_sb[:, 4 + c:5 + c]),
                           reads=[("cv", sl, c), "cvec_sb"], writes=[("ob", o)])
                        dma("sp", yT[256 + c * 128:256 + (c + 1) * 128, cols], ob[o][:], reads=[("ob", o)],
                            writes=["yT"])
                return p0, p1, p2

            ya_toks = []
            cc_sem_a = nc.alloc_semaphore(f"cc_sema{l}")
            ab_load(0)
            ab_stats(0)
            for s in range(4):
                ab_sub(0, s)
            if NT > 1:
                ab_load(1)
            for t in range(NT):
                hooks = {}

                def add(gi, f):
                    hooks.setdefault(gi, []).append(f)
                if t >= 1:
                    p0, p1, p2 = c2_pieces(t - 1)
                    add(1, p0)
                    add(6, p1)
                    add(11, p2)
                if t + 1 < NT:
                    add(3, lambda t=t: ab_stats(t + 1))
                    for s in range(4):
                        add(11 + 4 * s, lambda t=t, s=s: ab_sub_a(t + 1, s))
                        add(15 + 4 * s, lambda t=t, s=s: ab_sub(t + 1, s, with_a=False))
                if t + 2 < NT:
                    add(13, lambda t=t: ab_load(t + 2, cs=False))
                c1(t, hooks)
                if t + 2 < NT:
                    ab_load(t + 2, x=False)
            for f in c2_pieces(NT - 1):
                f()
            sch.collective(ya_toks, lambda e: e.collective_compute(
                "AllGather", ALU.bypass, replica_groups=PAIRS,
                ins=[yasrc[l][:, :]], outs=[yadst[l][:, :]]), cc_sem_a)
            sch.end_phase()
        if l == 0:
            es_w0.close()


        with ExitStack() as es:
            def sb(name, shape, dt=F32):
                return es.enter_context(nc.sbuf_tensor(f"{name}_L{l}", list(shape), dt))

            def ps(name, shape, dt=F32):
                return es.enter_context(nc.psum_tensor(f"{name}_L{l}", list(shape), dt))
            ksb = [sb(f"ksb{i}", [128, S], BF16) for i in range(2)]
            vsb = [sb(f"vsb{i}", [128, NKB, 128], BF16) for i in range(2)]
            qsb = [sb(f"qsb{i}", [128, 512], BF16) for i in range(3)]
            pb = [sb(f"pb{i}", [128, 2, 512], BF16) for i in range(3)]
            trif = sb("trif", [128, 128])
            tri2 = sb("tri2", [128, 2, 128], BF16)
            ones_b = sb("ones_b", [128, 128], BF16)
            onesE = sb("onesE", [128, 128])
            ones_f = sb("ones_f", [128, 128])
            acc2 = [sb(f"acc2_{i}", [128, 512]) for i in range(2)]
            epsc = sb("epsc", [128, 1])
            lam_sb = sb("lam_sb", [128, 256])
            lprod = sb("lprod", [128, 2, 64])
            lsum = sb("lsum", [128, 2])
            nlam = sb("nlam", [128, 1])
            sg_sb = sb("sg_sb", [128, 1])
            osb = [sb(f"osb{i}", [128, 2, 512]) for i in range(2)]
            lsb = [sb(f"lsb{i}", [128, 2, 512]) for i in range(2)]
            o1 = [sb(f"o1_{i}", [128, 512]) for i in range(2)]
            osq = [sb(f"osq{i}", [128, 512]) for i in range(2)]
            rstd = [sb(f"rstd2_{i}", [128, 512]) for i in range(2)]
            yb = [sb(f"yb{i}", [128, 512], BF16) for i in range(2)]
            ps_s = [ps(f"ps_s{i}", [128, 2, 512]) for i in range(2)]
            ps_o = ps("ps_o", [128, 2, 512])
            ps_l = ps("ps_l", [128, 2, 512])

            dma("sp", lam_sb[:], lamv[l], writes=["lam_sb"])
            dma("sp", sg_sb[:], subg[l], writes=["sg_sb"])
            op("pool", lambda e: e.memset(trif[:], 1.0), writes=["trif"])
            op("pool", lambda e: e.affine_select(out=trif[:], in_=trif[:], pattern=[[1, 128]],
                                                 compare_op=ALU.is_ge, fill=0.0, base=0,
                                                 channel_multiplier=-1), reads=["trif"], writes=["trif"])
            for mm in range(2):
                op("dve", lambda e, mm=mm: e.tensor_copy(out=tri2[:, mm, :], in_=trif[:]), reads=["trif"], writes=["tri2"])
            op("pool", lambda e: e.memset(ones_b[:], 1.0), writes=["ones_b"])
            op("pool", lambda e: e.memset(onesE[:], 1.0 / 128.0), writes=["onesE"])
            op("pool", lambda e: e.memset(ones_f[:], 1.0), writes=["ones_f"])
            op("pool", lambda e: e.memset(epsc[:], RMS_EPS), writes=["epsc"])
            lv = lam_sb[:].rearrange("p (a d) -> p a d", a=4)
            for i in range(2):
                op("dve", lambda e, i=i: e.tensor_tensor(out=lprod[:, i, :], in0=lv[:, 2 * i, :], in1=lv[:, 2 * i + 1, :],
                                                         op=ALU.mult), reads=["lam_sb"], writes=["lprod"])
            op("dve", lambda e: e.tensor_reduce(out=lsum[:], in_=lprod[:], axis=mybir.AxisListType.X, op=ALU.add),
               reads=["lprod"], writes=["lsum"])
            op("act", lambda e: e.activation(out=lsum[:], in_=lsum[:], func=AF.Exp), reads=["lsum"], writes=["lsum"])
            op("dve", lambda e: e.tensor_tensor(out=nlam[:], in0=lsum[:, 1:2], in1=lsum[:, 0:1], op=ALU.subtract),
               reads=["lsum"], writes=["nlam"])
            op("dve", lambda e: e.tensor_scalar(out=nlam[:], in0=nlam[:], scalar1=-lam_init, scalar2=None, op0=ALU.add),
               reads=["nlam"], writes=["nlam"])
            op("dve", lambda e: e.tensor_scalar(out=sg_sb[:], in0=sg_sb[:], scalar1=1.0 - lam_init, scalar2=None,
                                                op0=ALU.mult), reads=["sg_sb"], writes=["sg_sb"])

            def load_head(h):
                hs = h % 2
                dma("sp", ksb[hs][:], kT[h], writes=[("ksb", hs)])
                dma("sp", vsb[hs][:], vS[:, h * 128:(h + 1) * 128].rearrange("(kb p) e -> p kb e", p=128),
                    writes=[("vsb", hs)])

            def load_q(gidx):
                h, g = divmod(gidx, NT)
                qs = gidx % 3
                dma("sp", qsb[qs][:], qT[h, :, g * 512:(g + 1) * 512], writes=[("qsb", qs)])

            steps = []
            for gidx in range(HL * NT):
                g = gidx % NT
                for kb in range(4 * g + 4):
                    steps.append((gidx, kb))
            slot_of = {}
            cnt = {"si": 0, "pi": 0, "step": 0}
            deferred = []

            def emit_S(i):
                gidx, kb = steps[i]
                h, g = divmod(gidx, NT)
                hs, qs = h % 2, gidx % 3
                j = kb - 4 * g
                q0 = 128 * j if j > 0 else 0
                s_ = cnt["si"] % 2
                cnt["si"] += 1
                slot_of[i] = s_
                for mm in range(2):
                    lo = 64 * mm
                    op("pe", lambda e, mm=mm, lo=lo: e.matmul(
                        ps_s[s_][:, mm, q0:512], lhsT=ksb[hs][lo:lo + 64, kb * 128:(kb + 1) * 128],
                        rhs=qsb[qs][lo:lo + 64, q0:512], start=True, stop=True),
                       reads=[("ksb", hs), ("qsb", qs)], writes=[("ps_s", s_)], signal=(mm == 1))

            def emit_rest(i):
                gidx, kb = steps[i]
                h, g = divmod(gidx, NT)
                hs = h % 2
                nkb = 4 * g + 4
                j = kb - 4 * g
                q0 = 128 * j if j > 0 else 0
                s_ = slot_of.pop(i)
                p_ = cnt["pi"] % 3
                cnt["pi"] += 1
                op("act", lambda e: e.activation(out=pb[p_][:, :, q0:512], in_=ps_s[s_][:, :, q0:512],
                                                 func=AF.Exp, scale=0.125),
                   reads=[("ps_s", s_)], writes=[("pb", p_)])
                if j >= 0:
                    op("dve", lambda e: e.tensor_tensor(out=pb[p_][:, :, q0:q0 + 128], in0=pb[p_][:, :, q0:q0 + 128],
                                                        in1=tri2[:], op=ALU.mult),
                       reads=[("pb", p_), "tri2"], writes=[("pb", p_)])
                run_deferred()
                if i + 2 < len(steps):
                    emit_S(i + 2)
                first = (kb == 0)
                lastk = (kb == nkb - 1)
                for mm in range(2):
                    op("pe", lambda e, mm=mm: e.matmul(ps_o[:, mm, q0:512], lhsT=vsb[hs][:, kb, :],
                                                       rhs=pb[p_][:, mm, q0:512], start=first, stop=lastk),
                       reads=[("vsb", hs), ("pb", p_)], writes=["ps_o"], signal=False)
                op("pe", lambda e: e.matmul(ps_l[:, 0, q0:512], lhsT=ones_b[:], rhs=pb[p_][:, 0, q0:512],
                                            start=first, stop=lastk),
                   reads=["ones_b", ("pb", p_)], writes=["ps_l"], signal=True)
                pr = gidx % 2
                if first:
                    op("dve", lambda e: e.tensor_copy(out=acc2[pr][:], in_=pb[p_][:, 1, :]),
                       reads=[("pb", p_)], writes=[("acc2", pr)])
                else:
                    op("dve", lambda e: e.tensor_tensor(out=acc2[pr][:, q0:512], in0=acc2[pr][:, q0:512],
                                                        in1=pb[p_][:, 1, q0:512], op=ALU.add),
                       reads=[("pb", p_), ("acc2", pr)], writes=[("acc2", pr)])
                if lastk:
                    op("pe", lambda e: e.matmul(ps_l[:, 1, :], lhsT=ones_f[:], rhs=acc2[pr][:], start=True, stop=True),
                       reads=["ones_f", ("acc2", pr)], writes=["ps_l"], signal=True)
                    epilogue(gidx)

            def epilogue(gidx):
                h, g = divmod(gidx, NT)
                pr = gidx % 2
                op("act", lambda e: e.activation(out=lsb[pr][:], in_=ps_l[:], func=AF.Ln),
                   reads=["ps_l"], writes=[("lsb", pr)])
                op("act", lambda e: e.activation(out=lsb[pr][:], in_=lsb[pr][:], func=AF.Exp, scale=-1.0),
                   reads=[("lsb", pr)], writes=[("lsb", pr)])
                op("dve", lambda e: e.tensor_tensor(out=osb[pr][:], in0=ps_o[:], in1=lsb[pr][:], op=ALU.mult),
                   reads=["ps_o", ("lsb", pr)], writes=[("osb", pr)])
                op("dve", lambda e: e.scalar_tensor_tensor(out=o1[pr][:], in0=osb[pr][:, 1, :], scalar=nlam[:, 0:1],
                                                           in1=osb[pr][:, 0, :], op0=ALU.mult, op1=ALU.add),
                   reads=[("osb", pr), "nlam"], writes=[("o1", pr)])
                op("dve", lambda e: e.tensor_tensor(out=osq[pr][:], in0=o1[pr][:], in1=o1[pr][:], op=ALU.mult),
                   reads=[("o1", pr)], writes=[("osq", pr)])

                def part_c1():
                    s_ = cnt["si"] % 2
                    op("pe", lambda e: e.matmul(ps_s[s_][:, 0, :], lhsT=onesE[:], rhs=osq[pr][:], start=True, stop=True),
                       reads=["onesE", ("osq", pr)], writes=[("ps_s", s_)])

                    def part_c2():
                        op("act", lambda e: e.activation(out=rstd[pr][:], in_=ps_s[s_][:, 0, :], func=AF.Ln,
                                                         bias=epsc[:, 0:1]),
                           reads=[("ps_s", s_), "epsc"], writes=[("rstd", pr)])
                        op("act", lambda e: e.activation(out=rstd[pr][:], in_=rstd[pr][:], func=AF.Exp, scale=-0.5),
                           reads=[("rstd", pr)], writes=[("rstd", pr)])
                        op("dve", lambda e: e.scalar_tensor_tensor(out=yb[pr][:], in0=o1[pr][:], scalar=sg_sb[:, 0:1],
                                                                   in1=rstd[pr][:], op0=ALU.mult, op1=ALU.mult),
                           reads=[("o1", pr), "sg_sb", ("rstd", pr)], writes=[("yb", pr)])
                        tk = dma("sp", ybsrc[l][h][:, g * 512:(g + 1) * 512], yb[pr][:],
                                 reads=[("yb", pr)], writes=["ybsrc"])
                        yb_toks[h].append(tk)
                        if len(yb_toks[h]) == NT:
                            sch.collective(yb_toks[h], lambda e, h=h: e.collective_compute(
                                "AllGather", ALU.bypass, replica_groups=PAIRS,
                                ins=[ybsrc[l][h][:, :]], outs=[ybdst[l][h][:, :]]), cc_sem)
                    part_c2()
                deferred.append((cnt["step"] + 10, part_c1))

            def run_deferred(force=False):
                while True:
                    due = [d for d in deferred if force or d[0] <= cnt["step"]]
                    if not due:
                        break
                    d = due[0]
                    deferred.remove(d)
                    d[1]()

            yb_toks = [[] for _ in range(HL)]
            cc_sem = nc.alloc_semaphore(f"cc_sem{l}")
            load_head(0)
            load_q(0)
            load_q(1)
            emit_S(0)
            emit_S(1)
            for i, (gidx, kb) in enumerate(steps):
                cnt["step"] = i
                h, g = divmod(gidx, NT)
                if kb == 0:
                    if g == 0 and h + 1 < HL:
                        load_head(h + 1)
                    if gidx + 2 < HL * NT:
                        load_q(gidx + 2)
                emit_rest(i)
            cnt["step"] += 1000
            run_deferred(force=True)
            sch.pool_wait(cc_sem_a, 1)
            sch.pool_wait(cc_sem, HL)
            op("pool", lambda e: e.memset(epsc[:], RMS_EPS), writes=["epsc"])
            sch.end_phase()

        with ExitStack() as es:
            def sb(name, shape, dt=F32):
                return es.enter_context(nc.sbuf_tensor(f"{name}_L{l}", list(shape), dt))

            def ps(name, shape, dt=F32):
                return es.enter_context(nc.psum_tensor(f"{name}_L{l}", list(shape), dt))
            TT = 256
            NT3 = (S // 2) // TT
            wo = sb("wo", [128, 8, D], BF16)
            wg = sb("wg", [128, 8, FF], BF16)
            wu = sb("wu", [128, 8, FF], BF16)
            wd = sb("wd", [128, NFC, D], BF16)
            g_sb = sb("g3_sb", [128, 8])
            gf_sb = sb("gf_sb", [128, D]) if last else None
            identf = sb("identf3", [128, 128])
            ident = sb("ident3", [128, 128], BF16)
            mh = sb("mh3", [128, 4])
            xt = [sb(f"x3_{i}", [128, 2, D]) for i in range(2)]
            yt = [sb(f"y3_{i}", [128, 8, TT], BF16) for i in range(2)]
            ytc = sb("y3c", [128, 8, TT], BF16)
            rf_sb = sb("rf_sb", [128, 2])
            ss = sb("ss3", [128, 2])
            rs = sb("rs3", [128, 2])
            hb = sb("hb3", [128, D], BF16)
            hT = [sb(f"hT3_{i}", [128, 8, TT], BF16) for i in range(2)]
            aT = sb("aT", [128, NFC, TT], BF16)
            sg = [sb(f"sg{i}", [128, TT]) for i in range(2)]
            pmix = ps("pmix", [128, 2, 512])
            pT = ps("pT3", [128, D], BF16)
            pgu = [ps(f"pgu{i}", [128, 2, TT]) for i in range(2)]
            pdn = ps("pdn", [128, 2, 512])

            for k in range(8):
                dma("pool", wo[:, k, :], w_out[l, k * 128:(k + 1) * 128, :], writes=[("wo", k)])
            for k in range(8):
                dma("pool", wg[:, k, :], w_gate[l, k * 128:(k + 1) * 128, :], writes=[("wg", k)])
                dma("pool", wu[:, k, :], w_up[l, k * 128:(k + 1) * 128, :], writes=[("wu", k)])
            for fc in range(NFC):
                dma("pool", wd[:, fc, :], w_down[l, fc * 128:(fc + 1) * 128, :], writes=[("wd", fc)])
            dma("sp", g_sb[:], gcol[:, 2 * l + 1, :], writes=["g_sb"])
            dma("sp", rf_sb[:], rflag[:, :], writes=["rf_sb"])
            if last:
                dma("sp", gf_sb[:], gfin[:, :], writes=["gf_sb"])
            op("pool", lambda e: e.memset(identf[:], 1.0), writes=["identf"])
            op("pool", lambda e: e.affine_select(out=identf[:], in_=identf[:], pattern=[[-1, 128]],
                                                 compare_op=ALU.is_equal, fill=0.0, base=0,
                                                 channel_multiplier=1), reads=["identf"], writes=["identf"])
            op("dve", lambda e: e.tensor_copy(out=ident[:], in_=identf[:]), reads=["identf"], writes=["ident"])
            op("pool", lambda e: e.memset(mh[:], -0.5), writes=["mh"])

            yv = yT.rearrange("(k p) n -> p k n", p=128)
            yav = yadst[l].ap().rearrange("(k p) n -> p k n", p=128)
            def load_x(t):
                sl = t % 2
                dma("sp", xt[sl][:], x_tile256(t), writes=[("xt", sl)])

            def load_y(t):
                sl = t % 2
                for hf, (dstt, dn) in enumerate(((yt[sl], ("yt", sl)), (ytc, "ytc"))):
                    cs3 = slice(hf * (S // 2) + t * TT, hf * (S // 2) + (t + 1) * TT)
                    dma("sp", dstt[:, 0:2, :], yav[:, :, cs3], writes=[dn])
                    for gh in range(2 * HL):
                        rk, hh = divmod(gh, HL)
                        dma("sp", dstt[:, 2 + gh, :], ybdst[l][hh][rk * 128:(rk + 1) * 128, cs3], writes=[dn])
                    dma("sp", dstt[:, 6:8, :], yv[:, 2:4, cs3], writes=[dn])

            def blend_y(t):
                sl = t % 2
                op("dve", lambda e: e.tensor_scalar(out=yt[sl][:], in0=yt[sl][:], scalar1=rf_sb[:, 0:1], scalar2=None,
                                                    op0=ALU.mult), reads=[("yt", sl), "rf_sb"], writes=[("yt", sl)])
                op("dve", lambda e: e.scalar_tensor_tensor(out=yt[sl][:], in0=ytc[:], scalar=rf_sb[:, 1:2], in1=yt[sl][:],
                                                           op0=ALU.mult, op1=ALU.add),
                   reads=["ytc", ("yt", sl), "rf_sb"], writes=[("yt", sl)])

            gcnt = [0]

            def A1(t, s):
                sl = t % 2
                for half in range(2):
                    for k in range(8):
                        op("pe", lambda e, k=k, half=half: e.matmul(
                            pmix[:, half, :], lhsT=yt[sl][:, k, s * 128:(s + 1) * 128],
                            rhs=wo[:, k, half * 512:(half + 1) * 512], start=(k == 0), stop=(k == 7)),
                           reads=[("yt", sl), ("wo", k)], writes=["pmix"], signal=(k == 7 and half == 1))
                op("dve", lambda e: e.tensor_tensor(out=xt[sl][:, s, :], in0=xt[sl][:, s, :],
                                                    in1=pmix[:].rearrange("p a b -> p (a b)"), op=ALU.add),
                   reads=[("xt", sl), "pmix"], writes=[("xt", sl)])
                op("act", lambda e: e.activation(out=hb[:], in_=xt[sl][:, s, :], func=AF.Square,
                                                 accum_out=ss[:, s:s + 1]),
                   reads=[("xt", sl)], writes=["hb", ("ss", s)])
                op("dve", lambda e: e.tensor_scalar(out=rs[:, s:s + 1], in0=ss[:, s:s + 1], scalar1=1.0 / D,
                                                    scalar2=RMS_EPS, op0=ALU.mult, op1=ALU.add),
                   reads=[("ss", s)], writes=[("rs", s)])
                op("pool", lambda e: e.tensor_tensor(out=rs[:, s:s + 1], in0=rs[:, s:s + 1], in1=mh[:, 0:1],
                                                     op=ALU.pow), reads=[("rs", s), "mh"], writes=[("rs", s)])
                op("dve", lambda e: e.tensor_scalar(out=hb[:], in0=xt[sl][:, s, :], scalar1=rs[:, s:s + 1],
                                                    scalar2=None, op0=ALU.mult),
                   reads=[("xt", sl), ("rs", s)], writes=["hb"])

            def A2(t, s):
                sl = t % 2
                for k in range(8):
                    op("pe", lambda e, k=k: e.transpose(out=pT[:, k * 128:(k + 1) * 128],
                                                        in_=hb[:, k * 128:(k + 1) * 128], identity=ident[:]),
                       reads=["hb", "ident"], writes=["pT"], signal=(k == 7))
                for k in range(8):
                    op("dve", lambda e, k=k: e.tensor_scalar(
                        out=hT[sl][:, k, s * 128:(s + 1) * 128], in0=pT[:, k * 128:(k + 1) * 128],
                        scalar1=g_sb[:, k:k + 1], scalar2=None, op0=ALU.mult),
                       reads=["pT", "g_sb"], writes=[("hT", sl)])

            def p3_tile(t, hooks):
                sl = t % 2
                for fc in range(NFC):
                    gs = gcnt[0] % 2
                    gcnt[0] += 1
                    for (a, wt_, wn) in ((0, wg, "wg"), (1, wu, "wu")):
                        for k in range(8):
                            op("pe", lambda e, k=k, a=a, wt_=wt_, fc=fc, gs=gs: e.matmul(
                                pgu[gs][:, a, :], lhsT=wt_[:, k, fc * 128:(fc + 1) * 128], rhs=hT[sl][:, k, :],
                                start=(k == 0), stop=(k == 7)),
                               reads=[("hT", sl), (wn, k)], writes=[("pgu", gs)], signal=(k == 7 and a == 1))
                    op("act", lambda e, gs=gs: e.activation(out=sg[gs][:], in_=pgu[gs][:, 0, :], func=AF.Silu),
                       reads=[("pgu", gs)], writes=[("sg", gs)])
                    op("dve", lambda e, gs=gs, fc=fc: e.tensor_tensor(out=aT[:, fc, :], in0=pgu[gs][:, 1, :],
                                                                      in1=sg[gs][:], op=ALU.mult),
                       reads=[("pgu", gs), ("sg", gs)], writes=["aT"])
                    for f in hooks.get(fc, []):
                        f()
                for s in range(2):
                    for half in range(2):
                        for fc in range(NFC):
                            op("pe", lambda e, fc=fc, s=s, half=half: e.matmul(
                                pdn[:, half, :], lhsT=aT[:, fc, s * 128:(s + 1) * 128],
                                rhs=wd[:, fc, half * 512:(half + 1) * 512], start=(fc == 0), stop=(fc == NFC - 1)),
                               reads=["aT", ("wd", fc)], writes=["pdn"], signal=(fc == NFC - 1 and half == 1))
                    op("dve", lambda e, s=s: e.tensor_tensor(out=xt[sl][:, s, :], in0=xt[sl][:, s, :],
                                                             in1=pdn[:].rearrange("p a b -> p (a b)"), op=ALU.add),
                       reads=[("xt", sl), "pdn"], writes=[("xt", sl)])
                    if last:
                        op("act", lambda e, s=s: e.activation(out=hb[:], in_=xt[sl][:, s, :], func=AF.Square,
                                                              accum_out=ss[:, s:s + 1]),
                           reads=[("xt", sl)], writes=["hb", ("ss", s)])
                        op("dve", lambda e, s=s: e.tensor_scalar(out=rs[:, s:s + 1], in0=ss[:, s:s + 1], scalar1=1.0 / D,
                                                                 scalar2=RMS_EPS, op0=ALU.mult, op1=ALU.add),
                           reads=[("ss", s)], writes=[("rs", s)])
                        op("pool", lambda e, s=s: e.tensor_tensor(out=rs[:, s:s + 1], in0=rs[:, s:s + 1], in1=mh[:, 0:1],
                                                                  op=ALU.pow), reads=[("rs", s), "mh"], writes=[("rs", s)])
                        op("dve", lambda e, s=s: e.scalar_tensor_tensor(out=xt[sl][:, s, :], in0=xt[sl][:, s, :],
                                                                        scalar=rs[:, s:s + 1], in1=gf_sb[:],
                                                                        op0=ALU.mult, op1=ALU.mult),
                           reads=[("xt", sl), ("rs", s), "gf_sb"], writes=[("xt", sl)])
                tk = dma("sp", x_tile256(t, store=True), xt[sl][:], reads=[("xt", sl)], writes=["xo"])
                if not last:
                    x_toks.append(tk)
                    per = XCH // 256
                    if (t + 1) % per == 0:
                        c = t // per
                        sch.collective(x_toks[-per:], lambda e, c=c: e.collective_compute(
                            "AllGather", ALU.bypass, replica_groups=PAIRS,
                            ins=[xsrc[l][c][:, :]], outs=[xg[l][c][:, :]]), cc_semx)

            x_toks = []
            cc_semx = nc.alloc_semaphore(f"cc_semx{l}")
            load_x(0)
            load_y(0)
            blend_y(0)
            for s in range(2):
                A1(0, s)
                A2(0, s)
            if NT3 > 1:
                load_y(1)
            for t in range(NT3):
                hooks = {}
                if t + 1 < NT3:
                    load_x(t + 1)
                    hooks[3] = [lambda t=t: blend_y(t + 1)]
                    hooks[7] = [lambda t=t: A1(t + 1, 0)]
                    hooks[10] = [lambda t=t: A2(t + 1, 0)]
                    hooks[13] = [lambda t=t: A1(t + 1, 1)]
                    hooks[16] = [lambda t=t: A2(t + 1, 1)]
                if t + 2 < NT3:
                    hooks.setdefault(8, []).append(lambda t=t: load_y(t + 2))
                p3_tile(t, hooks)
            if not last:
                sch.pool_wait(cc_semx, NXC)
                op("pool", lambda e: e.memset(mh[:], -0.5), writes=["mh"])
            sch.end_phase()

    return nc


def _prep(inputs, S):
    f32 = np.float32
    depth = inputs["w_in"].shape[0]
    offs = np.cumsum([0, 256, 256, 256, 512, 512, 512, 512])
    a0, q0, k0, v0, c0 = 0, offs[3], offs[4], offs[5], offs[6]
    idx = list(range(c0, c0 + 512))

    def head_cols(base, heads):
        cols = []
        for h in heads:
            for mm in range(2):
                st = base + mm * 256 + h * 64
                cols += list(range(st, st + 64))
        return cols
    w_in_rank = []
    for r in range(2):
        heads = [HL * r + i for i in range(HL)]
        ida = [a0 + part * 256 + r * 128 + i for part in range(3) for i in range(128)]
        idr = ida + list(idx) + head_cols(q0, heads) + head_cols(k0, heads)
        for h in heads:
            idr += list(range(v0 + h * 128, v0 + (h + 1) * 128))
        idr = np.asarray(idr)
        assert idr.size == WIN_EXT
        w_in_rank.append(np.ascontiguousarray(np.asarray(inputs["w_in"], f32)[:, :, idr]))
    w_in_ext = w_in_rank[0]

    def cols128(v):
        v = np.asarray(v, f32)
        return np.ascontiguousarray(v.reshape(-1, 128).T)

    gcol = np.stack([cols128(inputs["mix_norm_g"][l]) if i == 0 else cols128(inputs["ffn_norm_g"][l])
                     for l in range(depth) for i in range(2)], axis=1)
    gfin = np.ascontiguousarray(np.broadcast_to(np.asarray(inputs["final_norm_g"], f32)[None, :], (128, D)))
    scw = np.stack([np.concatenate([np.stack([np.asarray(inputs["short_conv_w"], f32)[l, j, c * 128:(c + 1) * 128]
                                              for j in range(3)], axis=1) for c in range(2)], axis=1)
                    for l in range(depth)])
    glub = np.stack([cols128(inputs["glu_b"][l]) for l in range(depth)])
    cdw = np.stack([np.concatenate([np.stack([np.asarray(inputs["conf_dw_w"], f32)[l, j, c * 128:(c + 1) * 128]
                                              for j in range(31)], axis=1) for c in range(2)], axis=1)
                    for l in range(depth)])
    cvec = np.stack([np.concatenate([cols128(inputs["conf_dw_b"][l]), cols128(inputs["conf_ln_g"][l]),
                                     cols128(inputs["conf_ln_b"][l])], axis=1) for l in range(depth)])
    lamv = np.stack([np.broadcast_to(np.concatenate([np.asarray(inputs[k], f32)[l] for k in
                                                     ("lam_q1", "lam_k1", "lam_q2", "lam_k2")])[None, :], (128, 256))
                     for l in range(depth)])
    subg = np.stack([np.asarray(inputs["diff_subln_g"], f32)[l].reshape(128, 1) for l in range(depth)])
    inv_freq = (1.0 / (np.float32(10000.0) ** (np.arange(0, 64, 2, dtype=np.float32) / np.float32(64)))).astype(f32)
    p = np.arange(128)
    cst = np.zeros((128, 4), f32)
    cst[:, 0] = inv_freq[p % 32]
    cst[:, 1] = np.where((p % 64) < 32, -1.0, 1.0)
    gtok = np.ascontiguousarray(np.broadcast_to(np.asarray(inputs["mix_norm_g"], f32)[:, None, :], (depth, 128, D)))
    shared = dict(cst=cst, gcol=np.ascontiguousarray(gcol), gfin=gfin, gtok=gtok, w_in=w_in_ext,
                  w_out=np.ascontiguousarray(np.asarray(inputs["w_out"], f32)),
                  w_gate=np.ascontiguousarray(np.asarray(inputs["w_gate"], f32)),
                  w_up=np.ascontiguousarray(np.asarray(inputs["w_up"], f32)),
                  w_down=np.ascontiguousarray(np.asarray(inputs["w_down"], f32)),
                  scw=np.ascontiguousarray(scw), glub=np.ascontiguousarray(glub), cdw=np.ascontiguousarray(cdw),
                  cvec=np.ascontiguousarray(cvec), lamv=np.ascontiguousarray(lamv), subg=np.ascontiguousarray(subg))
    x = np.asarray(inputs["x"], f32)
    pos = np.asarray(inputs["positions"]).astype(np.int32)
    maps = []
    for b in range(x.shape[0]):
        xb = np.ascontiguousarray(x[b])
        pb_ = np.ascontiguousarray(np.broadcast_to(pos[b][None, :], (128, S)))
        for r in range(2):
            mp = dict(shared)
            mp["w_in"] = w_in_rank[r]
            mp["scw"] = np.ascontiguousarray(scw[:, :, 3 * r:3 * r + 3])
            mp["x"] = xb
            mp["xh"] = np.ascontiguousarray(xb[r * (S // 2):(r + 1) * (S // 2)])
            fl = np.zeros((128, 2), f32)
            fl[:, r] = 1.0
            mp["rflag"] = fl
            mp["pos"] = pb_
            maps.append(mp)
    return maps


_NC_CACHE = {}


def kernel(**inputs):
    x = np.asarray(inputs["x"])
    B, S, _ = x.shape
    depth = inputs["w_in"].shape[0]
    maps = _prep(inputs, S)
    n_cores = 8
    in_maps = [maps[c % (2 * B)] for c in range(n_cores)]
    key = (S, depth)
    if key not in _NC_CACHE:
        _NC_CACHE[key] = build_nc(S, depth)
    nc = _NC_CACHE[key]
    res = run_bass_kernel_spmd(nc, in_maps, core_ids=list(range(n_cores)))
    return np.stack([np.concatenate([np.asarray(res.results[2 * b + r]["out"], np.float32) for r in range(2)], axis=0)
                     for b in range(B)], axis=0)
```
